# Optimizing a Trainium2 kernel written in Bass

```python
import math
import jax, jax.numpy as jnp
from jax import lax
import numpy as np

D_MODEL = 2048
BATCH = 2
SEQ = 4096
DEPTH = 4
DEC_BATCH = 8
DEC_SEQ = 2048
PAST_LEN = 128

HEAD_DIM = 64
V_DIM = 2 * HEAD_DIM
ATTN_WIDTH = D_MODEL // 2
N_HEADS = ATTN_WIDTH // V_DIM
FOURIER_WIDTH = D_MODEL - ATTN_WIDTH
N_FGROUPS = 4
FGROUP_DIM = FOURIER_WIDTH // N_FGROUPS
AB_IN_WIDTH = 3 * ATTN_WIDTH + FOURIER_WIDTH
CONV_DIM = D_MODEL
CONV_K = 3
D_FF = (11 * D_MODEL) // 4
NUM_BUCKETS = 32
MAX_DISTANCE = 128
Q_BLOCK = 128
N_AB_LAYERS = (DEPTH + 1) // 2
N_C_LAYERS = DEPTH // 2
EPS = 1e-6

kernel_name = "hybrid_diffattn_fnet_shortconv_encoder"


def rms_norm(x, g):
    xf = x.astype(jnp.float32)
    xf = xf * lax.rsqrt(jnp.mean(xf * xf, axis=-1, keepdims=True) + EPS)
    return (xf * g.astype(jnp.float32)).astype(x.dtype)


def dwconv3(x, w):
    xp = jnp.pad(x, ((0, 0), (1, 1), (0, 0)))
    return xp[:, :-2] * w[0] + xp[:, 1:-1] * w[1] + xp[:, 2:] * w[2]


def rel_bucket(rel):
    nb = NUM_BUCKETS // 2
    ret = jnp.where(rel > 0, nb, 0)
    n = jnp.abs(rel)
    max_exact = nb // 2
    nf = jnp.maximum(n, 1).astype(jnp.float32)
    large = max_exact + (jnp.log(nf / max_exact) / math.log(MAX_DISTANCE / max_exact)
                         * (nb - max_exact)).astype(jnp.int32)
    large = jnp.minimum(large, nb - 1)
    return ret + jnp.where(n < max_exact, n, large)


def lambda_init_fn(layer_idx):
    return 0.8 - 0.6 * math.exp(-0.3 * layer_idx)


def diff_attention(q, k, v, rel_bias, lam):
    B, S = q.shape[0], q.shape[1]
    nblk = S // Q_BLOCK
    qb = q.reshape(B, nblk, Q_BLOCK, N_HEADS, 2, HEAD_DIM).transpose(1, 0, 2, 3, 4, 5)
    starts = jnp.arange(nblk, dtype=jnp.int32) * Q_BLOCK
    kpos = jnp.arange(S, dtype=jnp.int32)
    scale = HEAD_DIM ** -0.5

    def one_block(args):
        qblk, start = args
        qpos = start + jnp.arange(Q_BLOCK, dtype=jnp.int32)
        bucket = rel_bucket(kpos[None, :] - qpos[:, None])
        bias = jnp.transpose(rel_bias[bucket], (2, 0, 1)).astype(jnp.float32)
        s = jnp.einsum('bqhmd,bkhmd->bhmqk', qblk, k).astype(jnp.float32) * scale
        s = s + bias[None, :, None]
        p = jax.nn.softmax(s, axis=-1)
        a = p[:, :, 0] - lam * p[:, :, 1]
        return jnp.einsum('bhqk,bkhe->bqhe', a.astype(v.dtype), v)

    o = lax.map(one_block, (qb, starts))
    return o.transpose(1, 0, 2, 3, 4).reshape(B, S, N_HEADS, V_DIM)


def mixer_ab(h, w_in, w_out, lq1, lk1, lq2, lk2, subln_g, rel_bias, lambda_init):
    B, S, _ = h.shape
    z = h @ w_in
    q = z[..., :ATTN_WIDTH].reshape(B, S, N_HEADS, 2, HEAD_DIM)
    k = z[..., ATTN_WIDTH:2 * ATTN_WIDTH].reshape(B, S, N_HEADS, 2, HEAD_DIM)
    v = z[..., 2 * ATTN_WIDTH:3 * ATTN_WIDTH].reshape(B, S, N_HEADS, V_DIM)
    f = z[..., 3 * ATTN_WIDTH:]
    lam = (jnp.exp(jnp.sum(lq1.astype(jnp.float32) * lk1.astype(jnp.float32)))
           - jnp.exp(jnp.sum(lq2.astype(jnp.float32) * lk2.astype(jnp.float32)))
           + lambda_init)
    o = diff_attention(q, k, v, rel_bias, lam)
    o = (rms_norm(o, subln_g) * (1.0 - lambda_init)).reshape(B, S, ATTN_WIDTH)
    fg = f.reshape(B, S, N_FGROUPS, FGROUP_DIM).astype(jnp.float32)
    fo = jnp.real(jnp.fft.fft2(fg, axes=(1, 3), norm='ortho'))
    fo = fo.astype(h.dtype).reshape(B, S, FOURIER_WIDTH)
    return jnp.concatenate([o, fo], axis=-1) @ w_out


def mixer_c(h, w_in, conv_w, w_out):
    z = h @ w_in
    bg, cg, xv = jnp.split(z, 3, axis=-1)
    return (bg * dwconv3(cg * xv, conv_w)) @ w_out


def conv_ffn(h, w_gate, w_up, conv_w, conv_b, w_down):
    g = dwconv3(h @ w_gate, conv_w) + conv_b
    return (jax.nn.silu(g) * (h @ w_up)) @ w_down


def run_trunk(x, rel_bias, norm_pre_mix, norm_post_mix, norm_pre_ffn, norm_post_ffn,
              ab_w_in, ab_w_out, ab_lambda_q1, ab_lambda_k1, ab_lambda_q2, ab_lambda_k2, ab_subln,
              c_w_in, c_conv, c_w_out, ffn_w_gate, ffn_w_up, ffn_conv, ffn_conv_b, ffn_w_down):
    for i in range(DEPTH):
        j = i // 2
        h = rms_norm(x, norm_pre_mix[i])
        if i % 2 == 0:
            m = mixer_ab(h, ab_w_in[j], ab_w_out[j], ab_lambda_q1[j], ab_lambda_k1[j],
                         ab_lambda_q2[j], ab_lambda_k2[j], ab_subln[j], rel_bias,
                         lambda_init_fn(i))
        else:
            m = mixer_c(h, c_w_in[j], c_conv[j], c_w_out[j])
        x = x + rms_norm(m, norm_post_mix[i])
        h = rms_norm(x, norm_pre_ffn[i])
        f = conv_ffn(h, ffn_w_gate[i], ffn_w_up[i], ffn_conv[i], ffn_conv_b[i], ffn_w_down[i])
        x = x + rms_norm(f, norm_post_ffn[i])
    return x


def setup_inputs(seed: int = 0) -> dict:
    key = jax.random.key(seed)
    ks = jax.random.split(key, 24)
    f32 = jnp.float32

    def nrm(k, shape, scale):
        return jax.random.normal(k, shape, f32) * scale

    def gain(k, shape):
        return 1.0 + 0.05 * jax.random.normal(k, shape, f32)

    return {
        "x_prompt": nrm(ks[0], (BATCH, SEQ, D_MODEL), 1.0),
        "x_sample": nrm(ks[1], (DEC_BATCH, DEC_SEQ, D_MODEL), 1.0),
        "rel_bias": nrm(ks[2], (NUM_BUCKETS, N_HEADS), 0.5),
        "norm_pre_mix": gain(ks[3], (DEPTH, D_MODEL)),
        "norm_post_mix": gain(ks[4], (DEPTH, D_MODEL)),
        "norm_pre_ffn": gain(ks[5], (DEPTH, D_MODEL)),
        "norm_post_ffn": gain(ks[6], (DEPTH, D_MODEL)),
        "ab_w_in": nrm(ks[7], (N_AB_LAYERS, D_MODEL, AB_IN_WIDTH), D_MODEL ** -0.5),
        "ab_w_out": nrm(ks[8], (N_AB_LAYERS, D_MODEL, D_MODEL), D_MODEL ** -0.5),
        "ab_lambda_q1": nrm(ks[9], (N_AB_LAYERS, HEAD_DIM), 0.1),
        "ab_lambda_k1": nrm(ks[10], (N_AB_LAYERS, HEAD_DIM), 0.1),
        "ab_lambda_q2": nrm(ks[11], (N_AB_LAYERS, HEAD_DIM), 0.1),
        "ab_lambda_k2": nrm(ks[12], (N_AB_LAYERS, HEAD_DIM), 0.1),
        "ab_subln": gain(ks[13], (N_AB_LAYERS, V_DIM)),
        "c_w_in": nrm(ks[14], (N_C_LAYERS, D_MODEL, 3 * CONV_DIM), D_MODEL ** -0.5),
        "c_conv": nrm(ks[15], (N_C_LAYERS, CONV_K, CONV_DIM), CONV_K ** -0.5),
        "c_w_out": nrm(ks[16], (N_C_LAYERS, CONV_DIM, D_MODEL), CONV_DIM ** -0.5),
        "ffn_w_gate": nrm(ks[17], (DEPTH, D_MODEL, D_FF), D_MODEL ** -0.5),
        "ffn_w_up": nrm(ks[18], (DEPTH, D_MODEL, D_FF), D_MODEL ** -0.5),
        "ffn_conv": nrm(ks[19], (DEPTH, CONV_K, D_FF), CONV_K ** -0.5),
        "ffn_conv_b": nrm(ks[20], (DEPTH, D_FF), 0.02),
        "ffn_w_down": nrm(ks[21], (DEPTH, D_FF, D_MODEL), D_FF ** -0.5),
    }


def reference(x_prompt, x_sample, rel_bias, norm_pre_mix, norm_post_mix, norm_pre_ffn, norm_post_ffn,
              ab_w_in, ab_w_out, ab_lambda_q1, ab_lambda_k1, ab_lambda_q2, ab_lambda_k2, ab_subln,
              c_w_in, c_conv, c_w_out, ffn_w_gate, ffn_w_up, ffn_conv, ffn_conv_b, ffn_w_down):
    y_prompt = run_trunk(x_prompt, rel_bias, norm_pre_mix, norm_post_mix, norm_pre_ffn, norm_post_ffn,
                         ab_w_in, ab_w_out, ab_lambda_q1, ab_lambda_k1, ab_lambda_q2, ab_lambda_k2,
                         ab_subln, c_w_in, c_conv, c_w_out, ffn_w_gate, ffn_w_up, ffn_conv,
                         ffn_conv_b, ffn_w_down)
    y_sample = run_trunk(x_sample, rel_bias, norm_pre_mix, norm_post_mix, norm_pre_ffn, norm_post_ffn,
                         ab_w_in, ab_w_out, ab_lambda_q1, ab_lambda_k1, ab_lambda_q2, ab_lambda_k2,
                         ab_subln, c_w_in, c_conv, c_w_out, ffn_w_gate, ffn_w_up, ffn_conv,
                         ffn_conv_b, ffn_w_down)
    return (y_prompt, y_sample)
```

```python
import math
from contextlib import ExitStack

import numpy as np
import ml_dtypes

import concourse.bass as bass
import concourse.mybir as mybir
from concourse.bass_utils import run_bass_kernel_spmd

F32 = mybir.dt.float32
BF16 = mybir.dt.bfloat16
ALU = mybir.AluOpType
AF = mybir.ActivationFunctionType
AX = mybir.AxisListType
NPBF = ml_dtypes.bfloat16

OPT = {"gain_evac": True, "stats": True, "use_rr": True}
EPS = 1e-6
GW = 1152
LU = GW + 127
NEG = -30000.0


class Cfg:
    def __init__(self, D=2048, T=4096, L=4, TWF=2048):
        self.D, self.T, self.L = D, T, L
        self.DC = D // 128
        self.AW = D // 2
        self.H = self.AW // 128
        self.FW = D - self.AW
        self.NG = self.FW // 256
        self.DFF = (11 * D) // 4
        self.FC = self.DFF // 128
        self.NTG = T // 512
        self.NKC = T // 128
        self.NAB = (L + 1) // 2
        self.NCL = L // 2
        self.TWF = min(TWF, T)
        self.NWF = T // self.TWF


class Buf:
    def __init__(self, name, t):
        self.name, self.t = name, t
        self.w = None
        self.r = {}
        self.ds = {}
        self.last_dma = {}

    def __getitem__(self, k):
        return self.t[k]


class DSem:
    def __init__(self, key, h):
        self.key, self.h, self.v = key, h, 0


class Eng:
    def __init__(self, key, h, sem):
        self.key, self.h, self.sem, self.n = key, h, sem, 0
        self.seen = {}


class K:
    def __init__(self, nc, es, ndsem=44):
        self.nc = nc
        self.es = es
        mk = lambda n: es.enter_context(nc.semaphore(n))
        self.eng = {
            "pe": Eng("pe", nc.tensor, mk("s_pe")),
            "act": Eng("act", nc.scalar, mk("s_act")),
            "dve": Eng("dve", nc.vector, mk("s_dve")),
            "pool": Eng("pool", nc.gpsimd, mk("s_pool")),
            "sp": Eng("sp", nc.sync, mk("s_sp")),
        }
        self.bar = mk("s_bar")
        self.barn = 0
        self.dsems = [DSem(f"d{i}", mk(f"s_d{i}")) for i in range(ndsem)]
        self.free_ds = {"sw": list(self.dsems[:12]), "hw": list(self.dsems[12:])}
        self.bufs = []
        self.nins = 0

    def sb(self, ctx, name, shape, dt):
        self.uid = getattr(self, "uid", 0) + 1
        t = ctx.enter_context(self.nc.sbuf_tensor(f"sb{self.uid}_{name}", list(shape), dt))
        b = Buf(name, t)
        self.bufs.append(b)
        return b

    def reg(self, name, t):
        b = Buf(name, t)
        self.bufs.append(b)
        return b

    def _ds(self, b, cls):
        if cls not in b.ds:
            b.ds[cls] = self.free_ds[cls].pop(0)
        return b.ds[cls]

    def _wait(self, E, tok):
        key, h, v = tok
        if E.key == "pe" and key == "pe":
            return
        if E.seen.get(key, 0) >= v:
            return
        E.seen[key] = v
        E.h.wait_ge(h, v)
        self.nins += 1

    def _deps(self, E, reads, writes):
        for b in reads:
            if b.w is not None:
                self._wait(E, b.w)
        for b in writes:
            if b.w is not None:
                self._wait(E, b.w)
            for key, (h, v) in b.r.items():
                self._wait(E, (key, h, v))

    def _mark(self, tok, reads, writes):
        key, h, v = tok
        for b in reads:
            old = b.r.get(key)
            if old is None or old[1] < v:
                b.r[key] = (h, v)
        for b in writes:
            b.w = tok
            b.r = {}

    def op(self, eng, fn, reads=(), writes=(), sig=True):
        E = self.eng[eng]
        self._deps(E, reads, writes)
        ins = fn(E.h)
        self.nins += 1
        if sig:
            E.n += 1
            ins.then_inc(E.sem, 1)
            tok = (E.key, E.sem, E.n)
        else:
            tok = (E.key, E.sem, E.n + 1)
        self._mark(tok, reads, writes)
        return tok

    def dma(self, q, parts, sbuf, reads=(), writes=(), slow=False):
        E = self.eng[q]
        cls = "sw" if q == "pool" else "hw"
        ds = self._ds(sbuf, cls)
        self._deps(E, reads, writes)
        for tk in sbuf.last_dma.values():
            self._wait(E, tk)
        for o, i in parts:
            if slow and tuple(o.shape)[-1] == 1:
                E.h.dma_start(out=o, in_=i, allow_slow_non_contiguous=True).then_inc(ds.h, 16)
            else:
                E.h.dma_start(out=o, in_=i).then_inc(ds.h, 16)
            ds.v += 16
            self.nins += 1
        tok = (ds.key, ds.h, ds.v)
        sbuf.last_dma[cls] = tok
        self._mark(tok, reads, writes)
        return tok

    def barrier(self):
        SP = self.eng["sp"]
        for k in ("pe", "act", "dve", "pool"):
            E = self.eng[k]
            if E.n > 0:
                self._wait(SP, (E.key, E.sem, E.n))
        for ds in self.dsems:
            if ds.v > 0:
                self._wait(SP, (ds.key, ds.h, ds.v))
        self.barn += 1
        self.nc.sync.sem_inc(self.bar, 1)
        for k in ("pe", "act", "dve", "pool"):
            self.eng[k].h.wait_ge(self.bar, self.barn)
        for b in self.bufs:
            b.w = None
            b.r = {}
            b.last_dma = {}

    def end_phase(self, local_bufs):
        self.barrier()
        for b in local_bufs:
            for cls, d_ in b.ds.items():
                self.free_ds[cls].append(d_)
            b.ds = {}
            self.bufs.remove(b)


class Rot:
    def __init__(self, items):
        self.items, self.i = list(items), 0

    def __call__(self):
        x = self.items[self.i % len(self.items)]
        self.i += 1
        return x


class QWin:
    def __init__(self, P, name, KC, W, halo=False):
        self.KC = KC
        self.W = W
        self.parts = [P.sb(f"{name}{i}", [128, KC, 512], BF16) for i in range(W // 512)]
        self.halo = P.sb(f"{name}h", [128, KC, 2], BF16) if halo else None

    def sel(self, lo):
        if lo >= self.W:
            return self.halo, self.W
        return self.parts[lo // 512], (lo // 512) * 512


class Phase:
    def __init__(self, k):
        self.k = k
        self.ctx = ExitStack()
        self.local = []

    def sb(self, name, shape, dt):
        b = self.k.sb(self.ctx, name, shape, dt)
        self.local.append(b)
        return b

    def sbn(self, name, n, shape, dt):
        return [self.sb(f"{name}{i}", shape, dt) for i in range(n)]

    def close(self):
        self.k.end_phase(self.local)
        self.ctx.close()


def lambda_init_fn(i):
    return 0.8 - 0.6 * math.exp(-0.3 * i)


def build(cfg):
    c = cfg
    D, T, L, DC, AW, H, FW, NG, DFF, FC, NTG, NKC = (c.D, c.T, c.L, c.DC, c.AW, c.H, c.FW, c.NG,
                                                     c.DFF, c.FC, c.NTG, c.NKC)
    NAB, NCL = c.NAB, c.NCL
    nc = bass.Bass("TRN2", target_bir_lowering=False)

    def din(name, shape, dt=F32):
        return nc.dram_tensor(name, list(shape), dt, kind="ExternalInput").ap()

    def dscr(name, shape, dt):
        return nc.dram_tensor(name, list(shape), dt).ap()

    xT = din("xT", [D, T])
    w_abin = din("w_abin", [NAB, D, 4 * AW])
    w_about = din("w_about", [NAB, D, D])
    w_cin = din("w_cin", [max(NCL, 1), D, 3 * D])
    w_cout = din("w_cout", [max(NCL, 1), D, D])
    w_gate = din("w_gate", [L, D, DFF])
    w_up = din("w_up", [L, D, DFF])
    w_down = din("w_down", [L, DFF, D])
    gains_d = din("gains", [128, 4 * L * DC])
    cconv_d = din("cconv", [128, max(NCL, 1) * DC * 3])
    fconv_d = din("fconv", [128, L * FC * 3])
    fconvb_d = din("fconvb", [128, L * FC])
    subln_d = din("subln", [128, NAB])
    lamin_d = din("lamin", [1, 4 * NAB * 64])
    relb_d = din("relb", [32, 8])
    oh_d = din("oh", [32, LU])
    csc_d = din("csc", [256, 512], BF16)
    dftc_d = din("dftc", [T, T], BF16)
    dfts_d = din("dfts", [T, T], BF16)
    ident_d = din("ident", [128, 128], BF16)
    keep_d = din("keep", [128, c.NWF + 1])
    amask_d = din("amask", [128, NKC * NTG])
    lmask_d = din("lmask", [128, NKC * NTG])
    rmask_d = din("rmask", [128, NKC * NTG])
    yT = nc.dram_tensor("yT", [D, T], F32, kind="ExternalOutput").ap()

    XS = dscr("XS", [T // 256, 128, DC, 256], F32)
    M2 = dscr("M2", [T // 256, 128, DC, 256], F32)
    HT = dscr("HT", [D, T], BF16)
    QT = dscr("QT", [AW, T], BF16)
    KT = dscr("KT", [AW, T], BF16)
    VV = dscr("VV", [T, AW], BF16)
    YY = dscr("YY", [NG * 2, 128, NKC, 256], BF16)
    MT = dscr("MT", [D, T], BF16)
    AT = dscr("AT", [DFF, T], BF16)
    UU = dscr("UU", [H, 128, LU], BF16)
    RR = dscr("RR", [128, T], F32)

    def fm(ap):
        return ap.rearrange("(c p) t -> p c t", p=128)

    es = ExitStack()
    with es:
        k = K(nc, es)
        ps = [k.reg(f"ps{i}", es.enter_context(nc.psum_tensor(f"ps{i}", [128, 512], F32))) for i in range(8)]

        G = ExitStack()
        es.enter_context(G)
        ones_bf = k.sb(G, "ones_bf", [128, 128], BF16)
        ident = k.sb(G, "ident", [128, 128], BF16)
        epscol = k.sb(G, "epscol", [128, 1], F32)
        gains = k.sb(G, "gains", [128, 4 * L * DC], F32)
        cconv = k.sb(G, "cconv", [128, max(NCL, 1) * DC * 3], F32)
        fconv = k.sb(G, "fconv", [128, L * FC * 3], F32)
        fconvb = k.sb(G, "fconvb", [128, L * FC], F32)
        subln = k.sb(G, "subln", [128, NAB], F32)
        keep = k.sb(G, "keep", [128, c.NWF + 1], F32)
        neglam = k.sb(G, "neglam", [128, NAB], F32)
        clr = k.sb(G, "clr", [128, 16], F32)
        csc = k.sb(G, "csc", [128, 2, 512], BF16)
        amask = k.sb(G, "amask", [128, NKC * NTG], F32)
        lmask = k.sb(G, "lmask", [128, NKC * NTG], F32)
        rmask = k.sb(G, "rmask", [128, NKC * NTG], F32)

        def gcol(kind, l, ch):
            i = (kind * L + l) * DC + ch
            return gains[:, i:i + 1]

        P = Phase(k)
        k.op("dve", lambda e: e.memset(ones_bf[:], 1.0), writes=[ones_bf])
        k.op("dve", lambda e: e.memset(epscol[:], EPS), writes=[epscol])
        for b, d in ((ident, ident_d), (gains, gains_d), (cconv, cconv_d), (fconv, fconv_d),
                     (fconvb, fconvb_d), (subln, subln_d), (keep, keep_d), (amask, amask_d),
                     (lmask, lmask_d), (rmask, rmask_d)):
            k.dma("sp", [(b[:], d)], b, writes=[b])
        k.dma("sp", [(csc[:], csc_d.rearrange("(c p) n -> p c n", p=128))], csc, writes=[csc])
        k.dma("sp", [(clr[:, 0:8], bass.AP(relb_d.tensor, 15 * 8, [[0, 128], [1, 8]])),
                     (clr[:, 8:16], bass.AP(relb_d.tensor, 31 * 8, [[0, 128], [1, 8]]))], clr, writes=[clr])
        lamin = P.sb("lamin", [128, 4 * NAB * 64], F32)
        k.dma("sp", [(lamin[:], bass.AP(lamin_d.tensor, 0, [[0, 128], [1, 4 * NAB * 64]]))], lamin, writes=[lamin])
        lprod = P.sb("lprod", [128, 64], F32)
        lsum = P.sb("lsum", [128, 2], F32)
        lexp = P.sb("lexp", [128, 2], F32)
        for j in range(NAB):
            for q in range(2):
                a0 = (2 * q) * NAB * 64 + j * 64
                b0 = (2 * q + 1) * NAB * 64 + j * 64
                k.op("dve", lambda e: e.tensor_tensor(out=lprod[:], in0=lamin[:, a0:a0 + 64],
                                                      in1=lamin[:, b0:b0 + 64], op=ALU.mult),
                     reads=[lamin], writes=[lprod])
                k.op("dve", lambda e: e.tensor_reduce(out=lsum[:, q:q + 1], in_=lprod[:], axis=AX.X, op=ALU.add),
                     reads=[lprod], writes=[lsum])
            k.op("act", lambda e: e.activation(out=lexp[:], in_=lsum[:], func=AF.Exp), reads=[lsum], writes=[lexp])
            k.op("dve", lambda e: e.tensor_tensor(out=neglam[:, j:j + 1], in0=lexp[:, 1:2], in1=lexp[:, 0:1],
                                                  op=ALU.subtract), reads=[lexp], writes=[neglam])
            li = lambda_init_fn(2 * j)
            k.op("dve", lambda e: e.tensor_single_scalar(out=neglam[:, j:j + 1], in_=neglam[:, j:j + 1], scalar=-li,
                                                         op=ALU.add), reads=[neglam], writes=[neglam])
        if NAB > 0:
            oh = P.sb("oh", [32, LU], F32)
            relb = P.sb("relb", [32, 8], F32)
            ones32 = P.sb("ones32", [32, 128], F32)
            lh = P.sbn("lh", 2, [32, 128], F32)
            trep = P.sbn("trep", 2, [128, LU], BF16)
            k.dma("sp", [(oh[:], oh_d)], oh, writes=[oh])
            k.dma("sp", [(relb[:], relb_d)], relb, writes=[relb])
            k.op("dve", lambda e: e.memset(ones32[:], 1.0), writes=[ones32])
            UUb = k.reg("UU", None)
            P.local.append(UUb)
            pr = Rot(ps)
            for h in range(H):
                l_ = lh[h % 2]
                tr = trep[h % 2]
                k.op("dve", lambda e: e.tensor_single_scalar(out=l_[:], in_=ones32[:], scalar=relb[:, h:h + 1],
                                                             op=ALU.mult),
                     reads=[ones32, relb], writes=[l_])
                for c0 in range(0, LU, 512):
                    n = min(512, LU - c0)
                    bk = pr()
                    k.op("pe", lambda e: e.matmul(bk[:, 0:n], lhsT=l_[:], rhs=oh[:, c0:c0 + n], start=True, stop=True),
                         reads=[l_, oh], writes=[bk])
                    k.op("act", lambda e: e.activation(out=tr[:, c0:c0 + n], in_=bk[:, 0:n], func=AF.Copy, scale=8.0),
                         reads=[bk], writes=[tr])
                k.dma("sp", [(UU[h], tr[:])], tr, reads=[tr], writes=[UUb])
        P.close()

        def evac(i, out_ap, in_ap, reads, writes):
            if i % 2 == 0:
                k.op("act", lambda e: e.activation(out=out_ap, in_=in_ap, func=AF.Copy), reads=reads, writes=writes)
            else:
                k.op("dve", lambda e: e.tensor_copy(out=out_ap, in_=in_ap), reads=reads, writes=writes)

        def load_window(P_, hres, src, w, TW, halo):
            t0 = w * TW
            for gi_, pb_ in enumerate(hres.parts):
                k.dma("pool", [(pb_[:], fm(src)[:, :, t0 + gi_ * 512:t0 + (gi_ + 1) * 512])], pb_, writes=[pb_])
                if halo and gi_ == 0:
                    hb_ = hres.halo
                    tl = max(t0 - 1, 0)
                    tr_ = min(t0 + TW, T - 1)
                    k.dma("pool", [(hb_[:, :, 0:1], fm(src)[:, :, tl:tl + 1]),
                                   (hb_[:, :, 1:2], fm(src)[:, :, tr_:tr_ + 1])], hb_, writes=[hb_], slow=True)
                    k.op("dve", lambda e: e.tensor_single_scalar(out=hb_[:, :, 0:1], in_=hb_[:, :, 0:1],
                                                                 scalar=keep[:, w:w + 1], op=ALU.mult),
                         reads=[hb_, keep], writes=[hb_])
                    k.op("dve", lambda e: e.tensor_single_scalar(out=hb_[:, :, 1:2], in_=hb_[:, :, 1:2],
                                                                 scalar=keep[:, w + 1:w + 2], op=ALU.mult),
                         reads=[hb_, keep], writes=[hb_])

        def norm_pass(xsrc, has_m, xdst, kpost, lpost, kpre, lpre):
            TGn = 256
            NS = 4
            P = Phase(k)
            xg = P.sbn("xg", NS, [128, DC, TGn], F32)
            mg = P.sbn("mg", NS, [128, DC, TGn], F32)
            sq = P.sbn("sq", NS, [128, DC, TGn], BF16)
            rs = P.sbn("rs", NS, [128, TGn], F32)
            rr = P.sbn("rr", NS, [128, TGn], F32)
            pr = Rot(ps)
            ng = T // TGn

            def xview(ap, i, tok):
                return ap[i] if ap is XS else fm(ap)[:, :, tok]

            def gb(kind, l):
                i0 = (kind * L + l) * DC
                return gains[:, i0:i0 + DC].unsqueeze(2).to_broadcast([128, DC, TGn])

            def rstd_gen(src_, sq_, rs_, rr_):
                k.op("act", lambda e: e.activation(out=sq_[:], in_=src_[:], func=AF.Square), reads=[src_], writes=[sq_])
                yield
                bk = pr()
                for ch in range(DC):
                    k.op("pe", lambda e: e.matmul(bk[:, 0:TGn], lhsT=ones_bf[:], rhs=sq_[:, ch, :],
                                                  start=(ch == 0), stop=(ch == DC - 1)),
                         reads=[sq_, ones_bf], writes=[bk], sig=(ch == DC - 1))
                yield
                k.op("act", lambda e: e.activation(out=rs_[:], in_=bk[:, 0:TGn], func=AF.Sqrt, bias=epscol[:],
                                                   scale=1.0 / D), reads=[bk, epscol], writes=[rs_])
                yield
                k.op("dve", lambda e: e.reciprocal(out=rr_[:], in_=rs_[:]), reads=[rs_], writes=[rr_])
                yield

            def load(i):
                s = i % NS
                tok = slice(i * TGn, (i + 1) * TGn)
                k.dma("sp", [(xg[s][:], xview(xsrc, i, tok))], xg[s], writes=[xg[s]])
                if has_m:
                    k.dma("sp", [(mg[s][:], M2[i])], mg[s], writes=[mg[s]])
                    if OPT["stats"] and OPT["use_rr"]:
                        k.dma("sp", [(rr[s][:], RR[:, tok])], rr[s], writes=[rr[s]])

            def group_gen(i, lane):
                s = i % NS
                tok = slice(i * TGn, (i + 1) * TGn)
                x_, m_, sq_, rs_, rr_ = xg[s], mg[s], sq[s], rs[s], rr[s]
                rrb = lambda: rr_[:].unsqueeze(1).to_broadcast([128, DC, TGn])
                if has_m:
                    if not (OPT["stats"] and OPT["use_rr"]):
                        yield from rstd_gen(m_, sq_, rs_, rr_)
                    if not OPT["gain_evac"]:
                        k.op("pool", lambda e: e.tensor_tensor(out=m_[:], in0=m_[:], in1=gb(kpost, lpost), op=ALU.mult),
                             reads=[m_, gains], writes=[m_])
                        yield
                    k.op("dve", lambda e: e.tensor_tensor(out=m_[:], in0=m_[:], in1=rrb(), op=ALU.mult),
                         reads=[m_, rr_], writes=[m_])
                    yield
                    k.op("dve" if lane == 0 else "pool",
                         lambda e: e.tensor_tensor(out=x_[:], in0=x_[:], in1=m_[:], op=ALU.add),
                         reads=[x_, m_], writes=[x_])
                    yield
                    k.dma("act", [(xview(xdst, i, tok), x_[:])], x_, reads=[x_])
                    yield
                if kpre is not None:
                    yield from rstd_gen(x_, sq_, rs_, rr_)
                    k.op("dve", lambda e: e.tensor_tensor(out=m_[:], in0=x_[:], in1=rrb(), op=ALU.mult),
                         reads=[x_, rr_], writes=[m_])
                    yield
                    if lane == 1:
                        for ch in range(DC):
                            k.op("act", lambda e: e.activation(out=sq_[:, ch, :], in_=m_[:, ch, :], func=AF.Copy,
                                                               scale=gcol(kpre, lpre, ch)), reads=[m_, gains], writes=[sq_])
                    else:
                        k.op("pool", lambda e: e.tensor_tensor(out=sq_[:], in0=m_[:], in1=gb(kpre, lpre), op=ALU.mult),
                             reads=[m_, gains], writes=[sq_])
                    yield
                    k.dma("act", [(fm(HT)[:, :, tok], sq_[:])], sq_, reads=[sq_])
                    yield

            NL = 2
            for i in range(min(NL, ng)):
                load(i)
            for p0 in range(0, ng, NL):
                for i in range(p0 + NL, min(p0 + 2 * NL, ng)):
                    load(i)
                gens = [group_gen(i, i - p0) for i in range(p0, min(p0 + NL, ng))]
                while gens:
                    for g_ in list(gens):
                        try:
                            next(g_)
                        except StopIteration:
                            gens.remove(g_)
            P.close()

        def gemm_out(W, A, Kd, kpost, lpost):
            KC = Kd // 128
            TW = min(T, 2048 if KC <= 16 else 1024)
            NGW = TW // 512
            P = Phase(k)
            ares = QWin(P, "ares", KC, TW)
            wblk = P.sbn("wblk", 2, [128, KC, 256], BF16)
            stage = P.sbn("stage", 4, [128, 512], F32)
            sqs = P.sbn("sqs", 4, [128, 512], BF16)
            rsw = P.sbn("rsw", 2, [128, 512], F32)
            accb = ps[0:NGW]
            pr, sr, wr, qr, rw = Rot(ps[NGW:8]), Rot(stage), Rot(wblk), Rot(sqs), Rot(rsw)
            ev = 0
            nblk = D // 256
            for w in range(T // TW):
                for gi_, pb_ in enumerate(ares.parts):
                    t0_ = w * TW + gi_ * 512
                    k.dma("pool", [(pb_[:], fm(A)[:, :, t0_:t0_ + 512])], pb_, writes=[pb_])
                pend = []
                for cb in range(nblk):
                    wb = wr()
                    k.dma("pool", [(wb[:], fm(W)[:, :, cb * 256:(cb + 1) * 256])], wb, writes=[wb])
                    for tg in range(NGW):
                        for mc in range(2):
                            chn = cb * 2 + mc
                            bk = pr()
                            for kc in range(KC):
                                k.op("pe", lambda e: e.matmul(bk[:], lhsT=wb[:, kc, mc * 128:(mc + 1) * 128],
                                                              rhs=ares.sel(tg * 512)[0][:, kc, (tg * 512) - ares.sel(tg * 512)[1]:((tg + 1) * 512) - ares.sel(tg * 512)[1]],
                                                              start=(kc == 0), stop=(kc == KC - 1)),
                                     reads=[wb, ares.sel(tg * 512)[0]], writes=[bk], sig=(kc == KC - 1))
                            while len(pend) > 1:
                                pend.pop(0)()
                            st = sr()
                            gc_ = gcol(kpost, lpost, chn)
                            if not OPT["gain_evac"]:
                                evac(ev, st[:], bk[:], [bk], [st])
                            elif ev % 2 == 0 or OPT["stats"]:
                                k.op("act", lambda e: e.activation(out=st[:], in_=bk[:], func=AF.Copy, scale=gc_),
                                     reads=[bk, gains], writes=[st])
                            else:
                                k.op("dve", lambda e: e.tensor_single_scalar(out=st[:], in_=bk[:], scalar=gc_, op=ALU.mult),
                                     reads=[bk, gains], writes=[st])
                            ev += 1
                            first = (chn == 0)
                            lastc = (chn == 2 * nblk - 1)
                            if OPT["stats"]:
                                sq_ = qr()
                                k.op("act", lambda e: e.activation(out=sq_[:], in_=bk[:], func=AF.Square),
                                     reads=[bk], writes=[sq_])

                                def red(sq_=sq_, tg=tg, first=first, lastc=lastc):
                                    k.op("pe", lambda e: e.matmul(accb[tg][:], lhsT=ones_bf[:], rhs=sq_[:], start=first,
                                                                  stop=lastc), reads=[ones_bf, sq_], writes=[accb[tg]],
                                         sig=True)
                                pend.append(red)
                            r0 = chn * 128
                            c0 = w * TW + tg * 512
                            k.dma("sp", [(M2[c0 // 256:c0 // 256 + 2, :, chn, :].rearrange("g p t -> p g t"),
                                          st[:].rearrange("p (g t) -> p g t", g=2))], st, reads=[st])
                while pend:
                    pend.pop(0)()
                for tg in range(NGW if OPT["stats"] else 0):
                    r_ = rw()
                    k.op("act", lambda e: e.activation(out=r_[:], in_=accb[tg][:], func=AF.Sqrt, bias=epscol[:],
                                                       scale=1.0 / D), reads=[accb[tg], epscol], writes=[r_])
                    k.op("dve", lambda e: e.reciprocal(out=r_[:], in_=r_[:]), reads=[r_], writes=[r_])
                    c0 = w * TW + tg * 512
                    k.dma("sp", [(RR[:, c0:c0 + 512], r_[:])], r_, reads=[r_])
            P.close()

        def ffn_in(l):
            TW, NW = c.TWF, c.NWF
            NGW = TW // 512
            P = Phase(k)
            hres = QWin(P, "hres", DC, TW, halo=True)
            wg = P.sbn("wg", 2, [128, DC, 256], BF16)
            wu = P.sbn("wu", 2, [128, DC, 256], BF16)
            gsb = P.sbn("gsb", 2, [128, TW + 2], F32)
            usb = P.sbn("usb", 2, [128, TW], F32)
            c1 = P.sbn("c1", 2, [128, TW], F32)
            ab = P.sbn("ab", 2, [128, TW], BF16)
            pr = Rot(ps)
            wi = 0
            si = 0
            for w in range(NW):
                load_window(P, hres, HT, w, TW, True)
                for cb in range((DFF + 255) // 256):
                    ncol = min(256, DFF - cb * 256)
                    wg_, wu_ = wg[wi % 2], wu[wi % 2]
                    wi += 1
                    k.dma("pool", [(wg_[:, :, 0:ncol], fm(w_gate[l])[:, :, cb * 256:cb * 256 + ncol])], wg_, writes=[wg_])
                    k.dma("pool", [(wu_[:, :, 0:ncol], fm(w_up[l])[:, :, cb * 256:cb * 256 + ncol])], wu_, writes=[wu_])
                    for mc in range(ncol // 128):
                        fc = cb * 2 + mc
                        g_, u_, c_, a_ = gsb[si % 2], usb[si % 2], c1[si % 2], ab[si % 2]
                        si += 1
                        for tg in range(NGW):
                            bg_ = pr()
                            for kc in range(DC):
                                k.op("pe", lambda e: e.matmul(bg_[:], lhsT=wg_[:, kc, mc * 128:(mc + 1) * 128],
                                                              rhs=hres.sel(tg * 512)[0][:, kc, (tg * 512) - hres.sel(tg * 512)[1]:((tg + 1) * 512) - hres.sel(tg * 512)[1]],
                                                              start=(kc == 0), stop=(kc == DC - 1)),
                                     reads=[wg_, hres.sel(tg * 512)[0]], writes=[bg_], sig=(kc == DC - 1))
                            k.op("act", lambda e: e.activation(out=g_[:, 1 + tg * 512:1 + (tg + 1) * 512], in_=bg_[:],
                                                               func=AF.Copy), reads=[bg_], writes=[g_])
                            bu_ = pr()
                            for kc in range(DC):
                                k.op("pe", lambda e: e.matmul(bu_[:], lhsT=wu_[:, kc, mc * 128:(mc + 1) * 128],
                                                              rhs=hres.sel(tg * 512)[0][:, kc, (tg * 512) - hres.sel(tg * 512)[1]:((tg + 1) * 512) - hres.sel(tg * 512)[1]],
                                                              start=(kc == 0), stop=(kc == DC - 1)),
                                     reads=[wu_, hres.sel(tg * 512)[0]], writes=[bu_], sig=(kc == DC - 1))
                            k.op("dve", lambda e: e.tensor_copy(out=u_[:, tg * 512:(tg + 1) * 512], in_=bu_[:]),
                                 reads=[bu_], writes=[u_])
                        bh_ = pr()
                        for kc in range(DC):
                            k.op("pe", lambda e: e.matmul(bh_[:, 0:2], lhsT=wg_[:, kc, mc * 128:(mc + 1) * 128],
                                                          rhs=hres.sel(TW)[0][:, kc, (TW) - hres.sel(TW)[1]:(TW + 2) - hres.sel(TW)[1]],
                                                          start=(kc == 0), stop=(kc == DC - 1)),
                                 reads=[wg_, hres.sel(TW)[0]], writes=[bh_], sig=(kc == DC - 1))
                        k.op("act", lambda e: e.activation(out=g_[:, 0:1], in_=bh_[:, 0:1], func=AF.Copy),
                             reads=[bh_], writes=[g_])
                        k.op("act", lambda e: e.activation(out=g_[:, TW + 1:TW + 2], in_=bh_[:, 1:2], func=AF.Copy),
                             reads=[bh_], writes=[g_])
                        cw = (l * FC + fc) * 3
                        bcol = fconvb[:, l * FC + fc:l * FC + fc + 1]
                        k.op("dve", lambda e: e.tensor_scalar(out=c_[:], in0=g_[:, 0:TW], scalar1=fconv[:, cw:cw + 1],
                                                              scalar2=bcol, op0=ALU.mult, op1=ALU.add),
                             reads=[g_, fconv, fconvb], writes=[c_])
                        k.op("dve", lambda e: e.scalar_tensor_tensor(out=c_[:], in0=g_[:, 1:TW + 1],
                                                                     scalar=fconv[:, cw + 1:cw + 2], in1=c_[:],
                                                                     op0=ALU.mult, op1=ALU.add),
                             reads=[g_, fconv, c_], writes=[c_])
                        k.op("dve", lambda e: e.scalar_tensor_tensor(out=c_[:], in0=g_[:, 2:TW + 2],
                                                                     scalar=fconv[:, cw + 2:cw + 3], in1=c_[:],
                                                                     op0=ALU.mult, op1=ALU.add),
                             reads=[g_, fconv, c_], writes=[c_])
                        k.op("act", lambda e: e.activation(out=c_[:], in_=c_[:], func=AF.Silu), reads=[c_], writes=[c_])
                        k.op("dve", lambda e: e.tensor_tensor(out=a_[:], in0=c_[:], in1=u_[:], op=ALU.mult),
                             reads=[c_, u_], writes=[a_])
                        k.dma("sp", [(AT[fc * 128:(fc + 1) * 128, w * TW:(w + 1) * TW], a_[:])], a_, reads=[a_])
            P.close()

        def c_in(j):
            TW, NW = c.TWF, c.NWF
            NGW = TW // 512
            P = Phase(k)
            hres = QWin(P, "hres", DC, TW, halo=True)
            w3 = P.sbn("w3", 2, [128, DC, 3, 256], BF16)
            psb = P.sbn("psb", 2, [128, TW + 2], F32)
            bgs = P.sbn("bgs", 2, [128, TW], F32)
            c1 = P.sbn("c1", 2, [128, TW], F32)
            ob = P.sbn("ob", 2, [128, TW], BF16)
            cgt = P.sbn("cgt", 2, [128, 512], F32)
            hct = P.sbn("hct", 2, [128, 2], F32)
            pr = Rot(ps)
            wi = si = ci = 0
            Wc = w_cin[j]
            for w in range(NW):
                load_window(P, hres, HT, w, TW, True)
                for cb in range(D // 256):
                    w_ = w3[wi % 2]
                    wi += 1
                    k.dma("pool", [(w_[:, :, q, :], fm(Wc)[:, :, q * D + cb * 256:q * D + (cb + 1) * 256])
                                   for q in range(3)], w_, writes=[w_])
                    for mc in range(2):
                        fcj = cb * 2 + mc
                        p_, b_, c_, o_ = psb[si % 2], bgs[si % 2], c1[si % 2], ob[si % 2]
                        hc_ = hct[si % 2]
                        si += 1
                        cs_ = slice(mc * 128, (mc + 1) * 128)
                        for tg in range(NGW):
                            ts_ = slice(tg * 512, (tg + 1) * 512)
                            bc_, bx_, bb_ = pr(), pr(), pr()
                            for q, bk in ((1, bc_), (2, bx_), (0, bb_)):
                                for kc in range(DC):
                                    k.op("pe", lambda e: e.matmul(bk[:], lhsT=w_[:, kc, q, cs_], rhs=hres.sel(ts_.start)[0][:, kc, (ts_.start) - hres.sel(ts_.start)[1]:(ts_.stop) - hres.sel(ts_.start)[1]],
                                                                  start=(kc == 0), stop=(kc == DC - 1)),
                                         reads=[w_, hres.sel(ts_.start)[0]], writes=[bk], sig=(kc == DC - 1))
                            ct = cgt[ci % 2]
                            ci += 1
                            k.op("act", lambda e: e.activation(out=ct[:], in_=bc_[:], func=AF.Copy), reads=[bc_], writes=[ct])
                            k.op("dve", lambda e: e.tensor_tensor(out=p_[:, 1 + tg * 512:1 + (tg + 1) * 512], in0=ct[:],
                                                                  in1=bx_[:], op=ALU.mult), reads=[ct, bx_], writes=[p_])
                            k.op("act", lambda e: e.activation(out=b_[:, ts_], in_=bb_[:], func=AF.Copy),
                                 reads=[bb_], writes=[b_])
                        bh_ = pr()
                        for q, c0 in ((1, 0), (2, 8)):
                            for kc in range(DC):
                                k.op("pe", lambda e: e.matmul(bh_[:, c0:c0 + 2], lhsT=w_[:, kc, q, cs_],
                                                              rhs=hres.sel(TW)[0][:, kc, (TW) - hres.sel(TW)[1]:(TW + 2) - hres.sel(TW)[1]],
                                                              start=(kc == 0), stop=(kc == DC - 1)),
                                     reads=[w_, hres.sel(TW)[0]], writes=[bh_], sig=(kc == DC - 1))
                        k.op("act", lambda e: e.activation(out=hc_[:], in_=bh_[:, 0:2], func=AF.Copy),
                             reads=[bh_], writes=[hc_])
                        k.op("dve", lambda e: e.tensor_tensor(out=p_[:, 0:1], in0=hc_[:, 0:1], in1=bh_[:, 8:9], op=ALU.mult),
                             reads=[hc_, bh_], writes=[p_])
                        k.op("dve", lambda e: e.tensor_tensor(out=p_[:, TW + 1:TW + 2], in0=hc_[:, 1:2], in1=bh_[:, 9:10],
                                                              op=ALU.mult), reads=[hc_, bh_], writes=[p_])
                        cw = (j * DC + fcj) * 3
                        k.op("dve", lambda e: e.tensor_single_scalar(out=c_[:], in_=p_[:, 0:TW], scalar=cconv[:, cw:cw + 1],
                                                                     op=ALU.mult), reads=[p_, cconv], writes=[c_])
                        k.op("dve", lambda e: e.scalar_tensor_tensor(out=c_[:], in0=p_[:, 1:TW + 1],
                                                                     scalar=cconv[:, cw + 1:cw + 2], in1=c_[:],
                                                                     op0=ALU.mult, op1=ALU.add),
                             reads=[p_, cconv, c_], writes=[c_])
                        k.op("dve", lambda e: e.scalar_tensor_tensor(out=c_[:], in0=p_[:, 2:TW + 2],
                                                                     scalar=cconv[:, cw + 2:cw + 3], in1=c_[:],
                                                                     op0=ALU.mult, op1=ALU.add),
                             reads=[p_, cconv, c_], writes=[c_])
                        k.op("dve", lambda e: e.tensor_tensor(out=o_[:], in0=c_[:], in1=b_[:], op=ALU.mult),
                             reads=[c_, b_], writes=[o_])
                        k.dma("sp", [(MT[fcj * 128:(fcj + 1) * 128, w * TW:(w + 1) * TW], o_[:])], o_, reads=[o_])
            P.close()

        def ab_in(j):
            TW = min(T, 2048)
            NGW = TW // 512
            P = Phase(k)
            hres = QWin(P, "hres", DC, TW)
            wblk = P.sbn("wblk", 2, [128, DC, 256], BF16)
            stq = P.sbn("stq", 4, [128, 512], BF16)
            stv = P.sbn("stv", 4, [128, 256], BF16)
            fsb = P.sbn("fsb", 2, [128, 2, 512], BF16)
            sty = P.sbn("sty", 4, [128, 512], BF16)
            pr, qr, vr, fr, yr, wr = Rot(ps), Rot(stq), Rot(stv), Rot(fsb), Rot(sty), Rot(wblk)
            ev = 0
            Wc = w_abin[j]
            for w in range(T // TW):
                load_window(P, hres, HT, w, TW, False)
                for cb in range(4 * AW // 256):
                    col0 = cb * 256
                    wb = wr()
                    k.dma("pool", [(wb[:], fm(Wc)[:, :, col0:col0 + 256])], wb, writes=[wb])
                    kind = col0 // AW
                    if kind in (0, 1):
                        dst = QT if kind == 0 else KT
                        for mc in range(2):
                            r0 = col0 - kind * AW + mc * 128
                            for tg in range(NGW):
                                bk = pr()
                                for kc in range(DC):
                                    k.op("pe", lambda e: e.matmul(bk[:], lhsT=wb[:, kc, mc * 128:(mc + 1) * 128],
                                                                  rhs=hres.sel(tg * 512)[0][:, kc, (tg * 512) - hres.sel(tg * 512)[1]:((tg + 1) * 512) - hres.sel(tg * 512)[1]],
                                                                  start=(kc == 0), stop=(kc == DC - 1)),
                                         reads=[wb, hres.sel(tg * 512)[0]], writes=[bk], sig=(kc == DC - 1))
                                st = qr()
                                evac(ev, st[:], bk[:], [bk], [st])
                                ev += 1
                                t0 = w * TW + tg * 512
                                k.dma("sp", [(dst[r0:r0 + 128, t0:t0 + 512], st[:])], st, reads=[st])
                    elif kind == 2:
                        v0 = col0 - 2 * AW
                        for tt in range(TW // 128):
                            bk = pr()
                            for kc in range(DC):
                                k.op("pe", lambda e: e.matmul(bk[:, 0:256], lhsT=hres.sel(tt * 128)[0][:, kc, (tt * 128) - hres.sel(tt * 128)[1]:((tt + 1) * 128) - hres.sel(tt * 128)[1]],
                                                              rhs=wb[:, kc, :], start=(kc == 0), stop=(kc == DC - 1)),
                                     reads=[wb, hres.sel(tt * 128)[0]], writes=[bk], sig=(kc == DC - 1))
                            st = vr()
                            evac(ev, st[:], bk[:, 0:256], [bk], [st])
                            ev += 1
                            t0 = w * TW + tt * 128
                            k.dma("sp", [(VV[t0:t0 + 128, v0:v0 + 256], st[:])], st, reads=[st])
                    else:
                        g = (col0 - 3 * AW) // 256
                        for tg in range(NGW):
                            f_ = fr()
                            for jj in range(2):
                                bk = pr()
                                for kc in range(DC):
                                    k.op("pe", lambda e: e.matmul(bk[:], lhsT=wb[:, kc, jj * 128:(jj + 1) * 128],
                                                                  rhs=hres.sel(tg * 512)[0][:, kc, (tg * 512) - hres.sel(tg * 512)[1]:((tg + 1) * 512) - hres.sel(tg * 512)[1]],
                                                                  start=(kc == 0), stop=(kc == DC - 1)),
                                         reads=[wb, hres.sel(tg * 512)[0]], writes=[bk], sig=(kc == DC - 1))
                                evac(ev, f_[:, jj, :], bk[:], [bk], [f_])
                                ev += 1
                            for tt in range(4):
                                bk = pr()
                                for jj in range(2):
                                    k.op("pe", lambda e: e.matmul(bk[:], lhsT=f_[:, jj, tt * 128:(tt + 1) * 128],
                                                                  rhs=csc[:, jj, :], start=(jj == 0), stop=(jj == 1)),
                                         reads=[f_, csc], writes=[bk], sig=(jj == 1))
                                st = yr()
                                evac(ev, st[:], bk[:], [bk], [st])
                                ev += 1
                                cc = (w * TW + tg * 512 + tt * 128) // 128
                                k.dma("sp", [(YY[g * 2:g * 2 + 2, :, cc, :].rearrange("a p b -> p a b"),
                                              st[:].rearrange("p (a b) -> p a b", a=2))], st, reads=[st])
            P.close()

        def attention(j):
            li = lambda_init_fn(2 * j)
            P = Phase(k)
            qh = P.sbn("qh", 2, [128, T], BF16)
            kh = P.sbn("kh", 2, [128, T], BF16)
            vh = P.sbn("vh", 2, [128, NKC, 128], BF16)
            Gt = P.sb("Gt", [128, H, GW], BF16)
            bc = P.sb("bc", [128, H * NKC * NTG], F32)
            Eb = P.sbn("Eb", 6, [128, 512], BF16)
            zs = [P.sbn("zs0_", 2, [128, 512], F32), P.sbn("zs1_", 2, [128, 512], F32)]
            osb = [P.sbn("os0_", 2, [128, 512], F32), P.sbn("os1_", 2, [128, 512], F32)]
            lnb = P.sbn("lnb", 2, [128, 512], F32)
            sqb = P.sbn("sqb", 2, [128, 512], BF16)
            obf = P.sbn("obf", 2, [128, 512], BF16)
            sr = Rot(ps[0:4])
            er = Rot(Eb)
            Ob = [ps[4], ps[5]]
            Zb = [ps[6], ps[7]]
            k.dma("sp", [(Gt[:, h, :], bass.AP(UU.tensor, h * 128 * LU + 127, [[LU - 1, 128], [1, GW]])) for h in range(H)],
                  Gt, writes=[Gt])
            NB = NKC * NTG
            for h in range(H):
                k.op("dve", lambda e: e.scalar_tensor_tensor(out=bc[:, h * NB:(h + 1) * NB], in0=lmask[:],
                                                             scalar=clr[:, h:h + 1], in1=amask[:],
                                                             op0=ALU.mult, op1=ALU.add),
                     reads=[lmask, clr, amask], writes=[bc])
                k.op("dve", lambda e: e.scalar_tensor_tensor(out=bc[:, h * NB:(h + 1) * NB], in0=rmask[:],
                                                             scalar=clr[:, 8 + h:9 + h], in1=bc[:, h * NB:(h + 1) * NB],
                                                             op0=ALU.mult, op1=ALU.add),
                     reads=[rmask, clr, bc], writes=[bc])

            def load_head(h):
                q_, k_, v_ = qh[h % 2], kh[h % 2], vh[h % 2]
                k.dma("sp", [(q_[:], QT[h * 128:(h + 1) * 128, :])], q_, writes=[q_])
                k.dma("sp", [(k_[:], KT[h * 128:(h + 1) * 128, :])], k_, writes=[k_])
                k.dma("sp", [(v_[:], VV.rearrange("(c p) e -> p c e", p=128)[:, :, h * 128:(h + 1) * 128])], v_, writes=[v_])

            fi = 0
            pending = []
            load_head(0)
            for h in range(H):
                q_, k_, v_ = qh[h % 2], kh[h % 2], vh[h % 2]
                for qg in range(NTG):
                    if qg == 1 and h + 1 < H:
                        load_head(h + 1)
                    ebufs = {}
                    for jx in range(NKC + 1):
                        if jx == min(6, NKC - 1) and pending:
                            pending.pop(0)()
                        if jx < NKC:
                            kc = jx
                            near = (4 * qg - 1 <= kc <= 4 * qg + 4)
                            sbks = [sr(), sr()]
                            for m in range(2):
                                ms = slice(m * 64, (m + 1) * 64)
                                sbk = sbks[m]
                                k.op("pe", lambda e: e.matmul(sbk[:], lhsT=k_[ms, kc * 128:(kc + 1) * 128],
                                                              rhs=q_[ms, qg * 512:(qg + 1) * 512], start=True,
                                                              stop=not near),
                                     reads=[k_, q_], writes=[sbk], sig=not near)
                            if near:
                                off = 512 - (128 * kc - 512 * qg)
                                for m in range(2):
                                    sbk = sbks[m]
                                    k.op("pe", lambda e: e.matmul(sbk[:], lhsT=ident[:], rhs=Gt[:, h, off:off + 512],
                                                                  start=False, stop=True),
                                         reads=[ident, Gt], writes=[sbk], sig=True)
                            bi = (h * NKC + kc) * NTG + qg
                            for m in range(2):
                                eb = er()
                                ebufs[(m, kc)] = eb
                                sbk = sbks[m]
                                k.op("act", lambda e: e.activation(out=eb[:], in_=sbk[:], func=AF.Exp,
                                                                   bias=bc[:, bi:bi + 1], scale=0.125),
                                     reads=[sbk, bc], writes=[eb])
                        if jx >= 1:
                            kc = jx - 1
                            for m in range(2):
                                eb = ebufs.pop((m, kc))
                                k.op("pe", lambda e: e.matmul(Ob[m][:], lhsT=v_[:, kc, :], rhs=eb[:], start=(kc == 0),
                                                              stop=(kc == NKC - 1)), reads=[v_, eb], writes=[Ob[m]],
                                     sig=(kc == NKC - 1))
                                k.op("pe", lambda e: e.matmul(Zb[m][:], lhsT=ones_bf[:], rhs=eb[:], start=(kc == 0),
                                                              stop=(kc == NKC - 1)), reads=[ones_bf, eb], writes=[Zb[m]],
                                     sig=True)
                    s_ = fi % 2
                    fi += 1
                    z0, z1, o0, o1 = zs[0][s_], zs[1][s_], osb[0][s_], osb[1][s_]
                    ln_, sq_, ob_ = lnb[s_], sqb[s_], obf[s_]
                    k.op("dve", lambda e: e.tensor_copy(out=z0[:], in_=Zb[0][:]), reads=[Zb[0]], writes=[z0])
                    k.op("dve", lambda e: e.tensor_copy(out=o0[:], in_=Ob[0][:]), reads=[Ob[0]], writes=[o0])
                    k.op("dve", lambda e: e.tensor_copy(out=z1[:], in_=Zb[1][:]), reads=[Zb[1]], writes=[z1])
                    k.op("dve", lambda e: e.tensor_copy(out=o1[:], in_=Ob[1][:]), reads=[Ob[1]], writes=[o1])
                    k.op("dve", lambda e: e.reciprocal(out=z0[:], in_=z0[:]), reads=[z0], writes=[z0])
                    k.op("dve", lambda e: e.reciprocal(out=z1[:], in_=z1[:]), reads=[z1], writes=[z1])
                    k.op("dve", lambda e: e.tensor_tensor(out=o0[:], in0=o0[:], in1=z0[:], op=ALU.mult),
                         reads=[o0, z0], writes=[o0])
                    k.op("dve", lambda e: e.tensor_tensor(out=o1[:], in0=o1[:], in1=z1[:], op=ALU.mult),
                         reads=[o1, z1], writes=[o1])
                    k.op("dve", lambda e: e.scalar_tensor_tensor(out=o0[:], in0=o1[:], scalar=neglam[:, j:j + 1],
                                                                 in1=o0[:], op0=ALU.mult, op1=ALU.add),
                         reads=[o1, neglam, o0], writes=[o0])
                    k.op("dve", lambda e: e.tensor_tensor(out=sq_[:], in0=o0[:], in1=o0[:], op=ALU.mult),
                         reads=[o0], writes=[sq_])

                    def part2(o0=o0, ln_=ln_, sq_=sq_, ob_=ob_, h=h, qg=qg):
                        sbk = sr()
                        k.op("pe", lambda e: e.matmul(sbk[:], lhsT=ones_bf[:], rhs=sq_[:], start=True, stop=True),
                             reads=[ones_bf, sq_], writes=[sbk])
                        k.op("act", lambda e: e.activation(out=ln_[:], in_=sbk[:], func=AF.Ln, bias=epscol[:],
                                                           scale=1.0 / 128), reads=[sbk, epscol], writes=[ln_])
                        k.op("act", lambda e: e.activation(out=ln_[:], in_=ln_[:], func=AF.Exp, scale=-0.5),
                             reads=[ln_], writes=[ln_])
                        k.op("dve", lambda e: e.tensor_tensor(out=o0[:], in0=o0[:], in1=ln_[:], op=ALU.mult),
                             reads=[o0, ln_], writes=[o0])
                        k.op("dve", lambda e: e.tensor_scalar(out=ob_[:], in0=o0[:], scalar1=subln[:, j:j + 1],
                                                              scalar2=float(1.0 - li), op0=ALU.mult, op1=ALU.mult),
                             reads=[o0, subln], writes=[ob_])
                        k.dma("sp", [(MT[h * 128:(h + 1) * 128, qg * 512:(qg + 1) * 512], ob_[:])], ob_, reads=[ob_])

                    pending.append(part2)
            while pending:
                pending.pop(0)()
            P.close()

        def seqdft(j):
            P = Phase(k)
            cs = P.sbn("cs", 2, [128, NKC, 512], BF16)
            ss = P.sbn("ss", 2, [128, NKC, 512], BF16)
            yp = P.sbn("yp", 3, [128, NKC, 2, 128], BF16)
            st4 = P.sbn("st", 4, [128, 512], BF16)
            pr, sr_, yr = Rot(ps), Rot(st4), Rot(yp)
            ev = 0
            for sbi in range(NTG):
                cs_, ss_ = cs[sbi % 2], ss[sbi % 2]
                k.dma("sp", [(cs_[:], fm(dftc_d)[:, :, sbi * 512:(sbi + 1) * 512])], cs_, writes=[cs_])
                k.dma("sp", [(ss_[:], fm(dfts_d)[:, :, sbi * 512:(sbi + 1) * 512])], ss_, writes=[ss_])
                for g in range(NG):
                    for jj in range(2):
                        y_ = yr()
                        k.dma("sp", [(y_[:], YY[g * 2 + jj].rearrange("p c (a b) -> p c a b", a=2))], y_, writes=[y_])
                        bk = pr()
                        for cc in range(NKC):
                            k.op("pe", lambda e: e.matmul(bk[:], lhsT=y_[:, cc, 0, :], rhs=cs_[:, cc, :],
                                                          start=(cc == 0), stop=False),
                                 reads=[y_, cs_], writes=[bk], sig=False)
                            k.op("pe", lambda e: e.matmul(bk[:], lhsT=y_[:, cc, 1, :], rhs=ss_[:, cc, :],
                                                          start=False, stop=(cc == NKC - 1)),
                                 reads=[y_, ss_], writes=[bk], sig=(cc == NKC - 1))
                        st = sr_()
                        evac(ev, st[:], bk[:], [bk], [st])
                        ev += 1
                        r0 = AW + (g * 2 + jj) * 128
                        k.dma("act", [(MT[r0:r0 + 128, sbi * 512:(sbi + 1) * 512], st[:])], st, reads=[st])
            P.close()

        norm_pass(xT, False, None, None, None, 0, 0)
        for i in range(L):
            j = i // 2
            if i % 2 == 0:
                ab_in(j)
                attention(j)
                seqdft(j)
                gemm_out(w_about[j], MT, D, 1, i)
            else:
                c_in(j)
                gemm_out(w_cout[j], MT, D, 1, i)
            norm_pass(xT if i == 0 else XS, True, XS, 1, i, 2, i)
            ffn_in(i)
            gemm_out(w_down[i], AT, DFF, 3, i)
            last = (i == L - 1)
            if last:
                norm_pass(XS, True, yT, 3, i, None, None)
            else:
                norm_pass(XS, True, XS, 3, i, 0, i + 1)
        nins = k.nins
    return nc, nins


def rel_bucket_np(rel):
    nb = 16
    ret = np.where(rel > 0, nb, 0)
    n = np.abs(rel)
    max_exact = nb // 2
    nf = np.maximum(n, 1).astype(np.float32)
    large = max_exact + (np.log(nf / np.float32(max_exact)) / np.float32(math.log(128 / max_exact))
                         * np.float32(nb - max_exact)).astype(np.int32)
    large = np.minimum(large, nb - 1)
    return ret + np.where(n < max_exact, n, large)


def core_consts(cfg, seqlens):
    T, NKC, NTG = cfg.T, cfg.NKC, cfg.NTG
    starts = np.cumsum([0] + list(seqlens))
    seq_of = np.zeros(T, np.int64)
    for si, (a, b) in enumerate(zip(starts[:-1], starts[1:])):
        seq_of[a:b] = si
    dc = np.zeros((T, T), np.float64)
    ds = np.zeros((T, T), np.float64)
    for a, S in zip(starts[:-1], seqlens):
        idx = np.arange(S, dtype=np.int64)
        m = (idx[:, None] * idx[None, :]) % S
        ang = 2.0 * np.pi * m / S
        dc[a:a + S, a:a + S] = np.cos(ang) / math.sqrt(S)
        ds[a:a + S, a:a + S] = np.sin(ang) / math.sqrt(S)
    am = np.zeros((NKC, NTG), np.float32)
    for kc in range(NKC):
        for qg in range(NTG):
            if seq_of[kc * 128] != seq_of[qg * 512]:
                am[kc, qg] = NEG
    keep = np.zeros(cfg.NWF + 1, np.float32)
    for b in range(1, cfg.NWF):
        t = b * cfg.TWF
        keep[b] = 1.0 if seq_of[t] == seq_of[t - 1] else 0.0
    rep = lambda v: np.ascontiguousarray(np.broadcast_to(v.reshape(1, -1), (128, v.size))).astype(np.float32)
    return {"dftc": dc.astype(NPBF), "dfts": ds.astype(NPBF), "amask": rep(am), "keep": rep(keep)}


def shared_consts(cfg):
    NKC, NTG = cfg.NKC, cfg.NTG
    lm = np.zeros((NKC, NTG), np.float32)
    rm = np.zeros((NKC, NTG), np.float32)
    for kc in range(NKC):
        for qg in range(NTG):
            if kc < 4 * qg - 1:
                lm[kc, qg] = 1.0
            if kc > 4 * qg + 4:
                rm[kc, qg] = 1.0
    rep = lambda v: np.ascontiguousarray(np.broadcast_to(v.reshape(1, -1), (128, v.size))).astype(np.float32)
    rel = 639 - np.arange(LU)
    bk = rel_bucket_np(rel)
    oh = (bk[None, :] == np.arange(32)[:, None]).astype(np.float32)
    cidx = np.arange(256, dtype=np.int64)
    ang = 2.0 * np.pi * ((cidx[:, None] * cidx[None, :]) % 256) / 256.0
    cc_, ss_ = np.cos(ang), -np.sin(ang)
    csc = np.concatenate([cc_[:, 0:128], ss_[:, 0:128], cc_[:, 128:256], ss_[:, 128:256]], axis=1) / 16.0
    return {"lmask": rep(lm), "rmask": rep(rm), "oh": oh, "csc": csc.astype(NPBF),
            "ident": np.eye(128, dtype=np.float32).astype(NPBF)}


def param_layout(cfg, p):
    L, DC, FC, NAB, NCL, D, DFF = cfg.L, cfg.DC, cfg.FC, cfg.NAB, cfg.NCL, cfg.D, cfg.DFF
    f = lambda a: np.ascontiguousarray(np.asarray(a, dtype=np.float32))
    gains = np.stack([f(p["norm_pre_mix"]), f(p["norm_post_mix"]), f(p["norm_pre_ffn"]), f(p["norm_post_ffn"])])
    gains = gains.reshape(4, L, DC, 128).transpose(3, 0, 1, 2).reshape(128, 4 * L * DC)
    cconv = f(p["c_conv"]).reshape(NCL, 3, DC, 128).transpose(3, 0, 2, 1).reshape(128, NCL * DC * 3)
    fconv = f(p["ffn_conv"]).reshape(L, 3, FC, 128).transpose(3, 0, 2, 1).reshape(128, L * FC * 3)
    fconvb = f(p["ffn_conv_b"]).reshape(L, FC, 128).transpose(2, 0, 1).reshape(128, L * FC)
    subln = f(p["ab_subln"]).T
    lamin = np.stack([f(p["ab_lambda_q1"]), f(p["ab_lambda_k1"]), f(p["ab_lambda_q2"]), f(p["ab_lambda_k2"])])
    relb = np.zeros((32, 8), np.float32)
    rb = f(p["rel_bias"])
    relb[:, :rb.shape[1]] = rb
    return {
        "w_abin": f(p["ab_w_in"]), "w_about": f(p["ab_w_out"]), "w_cin": f(p["c_w_in"]), "w_cout": f(p["c_w_out"]),
        "w_gate": f(p["ffn_w_gate"]), "w_up": f(p["ffn_w_up"]), "w_down": f(p["ffn_w_down"]),
        "gains": f(gains), "cconv": f(cconv), "fconv": f(fconv), "fconvb": f(fconvb), "subln": f(subln),
        "lamin": f(lamin.reshape(1, -1)), "relb": relb,
    }


def run(cfg, params, core_x, core_seqlens):
    nc, nins = build(cfg)
    base = dict(param_layout(cfg, params))
    base.update(shared_consts(cfg))
    cc_cache = {}
    in_maps = []
    for x, sl in zip(core_x, core_seqlens):
        key = tuple(sl)
        if key not in cc_cache:
            cc_cache[key] = core_consts(cfg, sl)
        m = dict(base)
        m.update(cc_cache[key])
        m["xT"] = np.ascontiguousarray(np.asarray(x, dtype=np.float32).T)
        in_maps.append(m)
    res = run_bass_kernel_spmd(nc, in_maps, core_ids=list(range(len(in_maps))))
    return [np.ascontiguousarray(r["yT"].T) for r in res.results]


def kernel(**inputs):
    xp = np.asarray(inputs["x_prompt"], dtype=np.float32)
    xs = np.asarray(inputs["x_sample"], dtype=np.float32)
    B, S, D = xp.shape
    DB, DS, _ = xs.shape
    L = int(np.asarray(inputs["norm_pre_mix"]).shape[0])
    cfg = Cfg(D=D, T=S, L=L, TWF=2048)
    assert 2 * DS == S and B == 2 and DB == 8
    core_x, core_sl = [], []
    for b in range(B):
        core_x.append(xp[b])
        core_sl.append([S])
    for i in range(DB // 2):
        core_x.append(np.concatenate([xs[2 * i], xs[2 * i + 1]], axis=0))
        core_sl.append([DS, DS])
    while len(core_x) < 8:
        core_x.append(np.zeros_like(core_x[-1]))
        core_sl.append(core_sl[-1])
    outs = run(cfg, inputs, core_x, core_sl)
    y_prompt = np.stack([outs[b] for b in range(B)]).astype(np.float32)
    ys = []
    for i in range(DB // 2):
        o = outs[B + i]
        ys.append(o[:DS])
        ys.append(o[DS:])
    y_sample = np.stack(ys).astype(np.float32)
    return (y_prompt, y_sample)
```

```python
import math
from contextlib import ExitStack

import numpy as np
import ml_dtypes

import concourse.bass as bass
import concourse.mybir as mybir
from concourse.bass_utils import run_bass_kernel_spmd

F32 = mybir.dt.float32
BF16 = mybir.dt.bfloat16
ALU = mybir.AluOpType
AF = mybir.ActivationFunctionType
AX = mybir.AxisListType
NPBF = ml_dtypes.bfloat16

OPT = {"gain_evac": True, "stats": True, "use_rr": True}
EPS = 1e-6
GW = 1152
LU = GW + 127
NEG = -30000.0


class Cfg:
    def __init__(self, D=2048, T=4096, L=4, TWF=2048):
        self.D, self.T, self.L = D, T, L
        self.DC = D // 128
        self.AW = D // 2
        self.H = self.AW // 128
        self.FW = D - self.AW
        self.NG = self.FW // 256
        self.DFF = (11 * D) // 4
        self.FC = self.DFF // 128
        self.NTG = T // 512
        self.NKC = T // 128
        self.NAB = (L + 1) // 2
        self.NCL = L // 2
        self.TWF = min(TWF, T)
        self.NWF = T // self.TWF


class Buf:
    def __init__(self, name, t):
        self.name, self.t = name, t
        self.w = None
        self.r = {}
        self.ds = {}
        self.last_dma = {}

    def __getitem__(self, k):
        return self.t[k]


class DSem:
    def __init__(self, key, h):
        self.key, self.h, self.v = key, h, 0


class Eng:
    def __init__(self, key, h, sem):
        self.key, self.h, self.sem, self.n = key, h, sem, 0
        self.seen = {}


class K:
    def __init__(self, nc, es, ndsem=44):
        self.nc = nc
        self.es = es
        mk = lambda n: es.enter_context(nc.semaphore(n))
        self.eng = {
            "pe": Eng("pe", nc.tensor, mk("s_pe")),
            "act": Eng("act", nc.scalar, mk("s_act")),
            "dve": Eng("dve", nc.vector, mk("s_dve")),
            "pool": Eng("pool", nc.gpsimd, mk("s_pool")),
            "sp": Eng("sp", nc.sync, mk("s_sp")),
        }
        self.bar = mk("s_bar")
        self.barn = 0
        self.dsems = [DSem(f"d{i}", mk(f"s_d{i}")) for i in range(ndsem)]
        self.free_ds = {"sw": list(self.dsems[:12]), "hw": list(self.dsems[12:])}
        self.bufs = []
        self.nins = 0

    def sb(self, ctx, name, shape, dt):
        self.uid = getattr(self, "uid", 0) + 1
        t = ctx.enter_context(self.nc.sbuf_tensor(f"sb{self.uid}_{name}", list(shape), dt))
        b = Buf(name, t)
        self.bufs.append(b)
        return b

    def reg(self, name, t):
        b = Buf(name, t)
        self.bufs.append(b)
        return b

    def _ds(self, b, cls):
        if cls not in b.ds:
            b.ds[cls] = self.free_ds[cls].pop(0)
        return b.ds[cls]

    def _wait(self, E, tok):
        key, h, v = tok
        if E.key == "pe" and key == "pe":
            return
        if E.seen.get(key, 0) >= v:
            return
        E.seen[key] = v
        E.h.wait_ge(h, v)
        self.nins += 1

    def _deps(self, E, reads, writes):
        for b in reads:
            if b.w is not None:
                self._wait(E, b.w)
        for b in writes:
            if b.w is not None:
                self._wait(E, b.w)
            for key, (h, v) in b.r.items():
                self._wait(E, (key, h, v))

    def _mark(self, tok, reads, writes):
        key, h, v = tok
        for b in reads:
            old = b.r.get(key)
            if old is None or old[1] < v:
                b.r[key] = (h, v)
        for b in writes:
            b.w = tok
            b.r = {}

    def op(self, eng, fn, reads=(), writes=(), sig=True):
        E = self.eng[eng]
        self._deps(E, reads, writes)
        ins = fn(E.h)
        self.nins += 1
        if sig:
            E.n += 1
            ins.then_inc(E.sem, 1)
            tok = (E.key, E.sem, E.n)
        else:
            tok = (E.key, E.sem, E.n + 1)
        self._mark(tok, reads, writes)
        return tok

    def dma(self, q, parts, sbuf, reads=(), writes=(), slow=False):
        E = self.eng[q]
        cls = "sw" if q == "pool" else "hw"
        ds = self._ds(sbuf, cls)
        self._deps(E, reads, writes)
        for tk in sbuf.last_dma.values():
            self._wait(E, tk)
        for o, i in parts:
            if slow and tuple(o.shape)[-1] == 1:
                E.h.dma_start(out=o, in_=i, allow_slow_non_contiguous=True).then_inc(ds.h, 16)
            else:
                E.h.dma_start(out=o, in_=i).then_inc(ds.h, 16)
            ds.v += 16
            self.nins += 1
        tok = (ds.key, ds.h, ds.v)
        sbuf.last_dma[cls] = tok
        self._mark(tok, reads, writes)
        return tok

    def barrier(self):
        SP = self.eng["sp"]
        for k in ("pe", "act", "dve", "pool"):
            E = self.eng[k]
            if E.n > 0:
                self._wait(SP, (E.key, E.sem, E.n))
        for ds in self.dsems:
            if ds.v > 0:
                self._wait(SP, (ds.key, ds.h, ds.v))
        self.barn += 1
        self.nc.sync.sem_inc(self.bar, 1)
        for k in ("pe", "act", "dve", "pool"):
            self.eng[k].h.wait_ge(self.bar, self.barn)
        for b in self.bufs:
            b.w = None
            b.r = {}
            b.last_dma = {}

    def end_phase(self, local_bufs):
        self.barrier()
        for b in local_bufs:
            for cls, d_ in b.ds.items():
                self.free_ds[cls].append(d_)
            b.ds = {}
            self.bufs.remove(b)


class Rot:
    def __init__(self, items):
        self.items, self.i = list(items), 0

    def __call__(self):
        x = self.items[self.i % len(self.items)]
        self.i += 1
        return x


class QWin:
    def __init__(self, P, name, KC, W, halo=False):
        self.KC = KC
        self.W = W
        self.parts = [P.sb(f"{name}{i}", [128, KC, 512], BF16) for i in range(W // 512)]
        self.halo = P.sb(f"{name}h", [128, KC, 2], BF16) if halo else None

    def sel(self, lo):
        if lo >= self.W:
            return self.halo, self.W
        return self.parts[lo // 512], (lo // 512) * 512


class Phase:
    def __init__(self, k):
        self.k = k
        self.ctx = ExitStack()
        self.local = []

    def sb(self, name, shape, dt):
        b = self.k.sb(self.ctx, name, shape, dt)
        self.local.append(b)
        return b

    def sbn(self, name, n, shape, dt):
        return [self.sb(f"{name}{i}", shape, dt) for i in range(n)]

    def close(self):
        self.k.end_phase(self.local)
        self.ctx.close()


def lambda_init_fn(i):
    return 0.8 - 0.6 * math.exp(-0.3 * i)


def build(cfg):
    c = cfg
    D, T, L, DC, AW, H, FW, NG, DFF, FC, NTG, NKC = (c.D, c.T, c.L, c.DC, c.AW, c.H, c.FW, c.NG,
                                                     c.DFF, c.FC, c.NTG, c.NKC)
    NAB, NCL = c.NAB, c.NCL
    nc = bass.Bass("TRN2", target_bir_lowering=False)

    def din(name, shape, dt=F32):
        return nc.dram_tensor(name, list(shape), dt, kind="ExternalInput").ap()

    def dscr(name, shape, dt):
        return nc.dram_tensor(name, list(shape), dt).ap()

    xT = din("xT", [D, T])
    w_abin = din("w_abin", [NAB, D, 4 * AW])
    w_about = din("w_about", [NAB, D, D])
    w_cin = din("w_cin", [max(NCL, 1), D, 3 * D])
    w_cout = din("w_cout", [max(NCL, 1), D, D])
    w_gate = din("w_gate", [L, D, DFF])
    w_up = din("w_up", [L, D, DFF])
    w_down = din("w_down", [L, DFF, D])
    gains_d = din("gains", [128, 4 * L * DC])
    cconv_d = din("cconv", [128, max(NCL, 1) * DC * 3])
    fconv_d = din("fconv", [128, L * FC * 3])
    fconvb_d = din("fconvb", [128, L * FC])
    subln_d = din("subln", [128, NAB])
    lamin_d = din("lamin", [1, 4 * NAB * 64])
    relb_d = din("relb", [32, 8])
    oh_d = din("oh", [32, LU])
    csc_d = din("csc", [256, 512], BF16)
    dftc_d = din("dftc", [T, T], BF16)
    dfts_d = din("dfts", [T, T], BF16)
    ident_d = din("ident", [128, 128], BF16)
    keep_d = din("keep", [128, c.NWF + 1])
    amask_d = din("amask", [128, NKC * NTG])
    lmask_d = din("lmask", [128, NKC * NTG])
    rmask_d = din("rmask", [128, NKC * NTG])
    yT = nc.dram_tensor("yT", [D, T], F32, kind="ExternalOutput").ap()

    XS = dscr("XS", [T // 256, 128, DC, 256], F32)
    M2 = dscr("M2", [T // 256, 128, DC, 256], F32)
    HT = dscr("HT", [D, T], BF16)
    QT = dscr("QT", [AW, T], BF16)
    KT = dscr("KT", [AW, T], BF16)
    VV = dscr("VV", [T, AW], BF16)
    YY = dscr("YY", [NG * 2, 128, NKC, 256], BF16)
    MT = dscr("MT", [D, T], BF16)
    AT = dscr("AT", [DFF, T], BF16)
    UU = dscr("UU", [H, 128, LU], BF16)
    RR = dscr("RR", [128, T], F32)

    def fm(ap):
        return ap.rearrange("(c p) t -> p c t", p=128)

    es = ExitStack()
    with es:
        k = K(nc, es)
        ps = [k.reg(f"ps{i}", es.enter_context(nc.psum_tensor(f"ps{i}", [128, 512], F32))) for i in range(8)]

        G = ExitStack()
        es.enter_context(G)
        ones_bf = k.sb(G, "ones_bf", [128, 128], BF16)
        ident = k.sb(G, "ident", [128, 128], BF16)
        epscol = k.sb(G, "epscol", [128, 1], F32)
        gains = k.sb(G, "gains", [128, 4 * L * DC], F32)
        cconv = k.sb(G, "cconv", [128, max(NCL, 1) * DC * 3], F32)
        fconv = k.sb(G, "fconv", [128, L * FC * 3], F32)
        fconvb = k.sb(G, "fconvb", [128, L * FC], F32)
        subln = k.sb(G, "subln", [128, NAB], F32)
        keep = k.sb(G, "keep", [128, c.NWF + 1], F32)
        neglam = k.sb(G, "neglam", [128, NAB], F32)
        clr = k.sb(G, "clr", [128, 16], F32)
        csc = k.sb(G, "csc", [128, 2, 512], BF16)
        amask = k.sb(G, "amask", [128, NKC * NTG], F32)
        lmask = k.sb(G, "lmask", [128, NKC * NTG], F32)
        rmask = k.sb(G, "rmask", [128, NKC * NTG], F32)

        def gcol(kind, l, ch):
            i = (kind * L + l) * DC + ch
            return gains[:, i:i + 1]

        P = Phase(k)
        k.op("dve", lambda e: e.memset(ones_bf[:], 1.0), writes=[ones_bf])
        k.op("dve", lambda e: e.memset(epscol[:], EPS), writes=[epscol])
        for b, d in ((ident, ident_d), (gains, gains_d), (cconv, cconv_d), (fconv, fconv_d),
                     (fconvb, fconvb_d), (subln, subln_d), (keep, keep_d), (amask, amask_d),
                     (lmask, lmask_d), (rmask, rmask_d)):
            k.dma("sp", [(b[:], d)], b, writes=[b])
        k.dma("sp", [(csc[:], csc_d.rearrange("(c p) n -> p c n", p=128))], csc, writes=[csc])
        k.dma("sp", [(clr[:, 0:8], bass.AP(relb_d.tensor, 15 * 8, [[0, 128], [1, 8]])),
                     (clr[:, 8:16], bass.AP(relb_d.tensor, 31 * 8, [[0, 128], [1, 8]]))], clr, writes=[clr])
        lamin = P.sb("lamin", [128, 4 * NAB * 64], F32)
        k.dma("sp", [(lamin[:], bass.AP(lamin_d.tensor, 0, [[0, 128], [1, 4 * NAB * 64]]))], lamin, writes=[lamin])
        lprod = P.sb("lprod", [128, 64], F32)
        lsum = P.sb("lsum", [128, 2], F32)
        lexp = P.sb("lexp", [128, 2], F32)
        for j in range(NAB):
            for q in range(2):
                a0 = (2 * q) * NAB * 64 + j * 64
                b0 = (2 * q + 1) * NAB * 64 + j * 64
                k.op("dve", lambda e: e.tensor_tensor(out=lprod[:], in0=lamin[:, a0:a0 + 64],
                                                      in1=lamin[:, b0:b0 + 64], op=ALU.mult),
                     reads=[lamin], writes=[lprod])
                k.op("dve", lambda e: e.tensor_reduce(out=lsum[:, q:q + 1], in_=lprod[:], axis=AX.X, op=ALU.add),
                     reads=[lprod], writes=[lsum])
            k.op("act", lambda e: e.activation(out=lexp[:], in_=lsum[:], func=AF.Exp), reads=[lsum], writes=[lexp])
            k.op("dve", lambda e: e.tensor_tensor(out=neglam[:, j:j + 1], in0=lexp[:, 1:2], in1=lexp[:, 0:1],
                                                  op=ALU.subtract), reads=[lexp], writes=[neglam])
            li = lambda_init_fn(2 * j)
            k.op("dve", lambda e: e.tensor_single_scalar(out=neglam[:, j:j + 1], in_=neglam[:, j:j + 1], scalar=-li,
                                                         op=ALU.add), reads=[neglam], writes=[neglam])
        if NAB > 0:
            oh = P.sb("oh", [32, LU], F32)
            relb = P.sb("relb", [32, 8], F32)
            ones32 = P.sb("ones32", [32, 128], F32)
            lh = P.sbn("lh", 2, [32, 128], F32)
            trep = P.sbn("trep", 2, [128, LU], BF16)
            k.dma("sp", [(oh[:], oh_d)], oh, writes=[oh])
            k.dma("sp", [(relb[:], relb_d)], relb, writes=[relb])
            k.op("dve", lambda e: e.memset(ones32[:], 1.0), writes=[ones32])
            UUb = k.reg("UU", None)
            P.local.append(UUb)
            pr = Rot(ps)
            for h in range(H):
                l_ = lh[h % 2]
                tr = trep[h % 2]
                k.op("dve", lambda e: e.tensor_single_scalar(out=l_[:], in_=ones32[:], scalar=relb[:, h:h + 1],
                                                             op=ALU.mult),
                     reads=[ones32, relb], writes=[l_])
                for c0 in range(0, LU, 512):
                    n = min(512, LU - c0)
                    bk = pr()
                    k.op("pe", lambda e: e.matmul(bk[:, 0:n], lhsT=l_[:], rhs=oh[:, c0:c0 + n], start=True, stop=True),
                         reads=[l_, oh], writes=[bk])
                    k.op("act", lambda e: e.activation(out=tr[:, c0:c0 + n], in_=bk[:, 0:n], func=AF.Copy, scale=8.0),
                         reads=[bk], writes=[tr])
                k.dma("sp", [(UU[h], tr[:])], tr, reads=[tr], writes=[UUb])
        P.close()

        def evac(i, out_ap, in_ap, reads, writes):
            if i % 2 == 0:
                k.op("act", lambda e: e.activation(out=out_ap, in_=in_ap, func=AF.Copy), reads=reads, writes=writes)
            else:
                k.op("dve", lambda e: e.tensor_copy(out=out_ap, in_=in_ap), reads=reads, writes=writes)

        def load_window(P_, hres, src, w, TW, halo, after_first=None):
            t0 = w * TW
            for gi_, pb_ in enumerate(hres.parts):
                if gi_ == 1 and after_first is not None:
                    after_first()
                k.dma("pool", [(pb_[:], fm(src)[:, :, t0 + gi_ * 512:t0 + (gi_ + 1) * 512])], pb_, writes=[pb_])
                if halo and gi_ == 0:
                    hb_ = hres.halo
                    tl = max(t0 - 1, 0)
                    tr_ = min(t0 + TW, T - 1)
                    k.dma("pool", [(hb_[:, :, 0:1], fm(src)[:, :, tl:tl + 1]),
                                   (hb_[:, :, 1:2], fm(src)[:, :, tr_:tr_ + 1])], hb_, writes=[hb_], slow=True)
                    k.op("dve", lambda e: e.tensor_single_scalar(out=hb_[:, :, 0:1], in_=hb_[:, :, 0:1],
                                                                 scalar=keep[:, w:w + 1], op=ALU.mult),
                         reads=[hb_, keep], writes=[hb_])
                    k.op("dve", lambda e: e.tensor_single_scalar(out=hb_[:, :, 1:2], in_=hb_[:, :, 1:2],
                                                                 scalar=keep[:, w + 1:w + 2], op=ALU.mult),
                         reads=[hb_, keep], writes=[hb_])
            if len(hres.parts) == 1 and after_first is not None:
                after_first()

        def norm_pass(xsrc, has_m, xdst, kpost, lpost, kpre, lpre):
            TGn = 256
            NS = 4
            P = Phase(k)
            xg = P.sbn("xg", NS, [128, DC, TGn], F32)
            mg = P.sbn("mg", NS, [128, DC, TGn], F32)
            sq = P.sbn("sq", NS, [128, DC, TGn], BF16)
            rs = P.sbn("rs", NS, [128, TGn], F32)
            rr = P.sbn("rr", NS, [128, TGn], F32)
            pr = Rot(ps)
            ng = T // TGn

            def xview(ap, i, tok):
                return ap[i] if ap is XS else fm(ap)[:, :, tok]

            def gb(kind, l):
                i0 = (kind * L + l) * DC
                return gains[:, i0:i0 + DC].unsqueeze(2).to_broadcast([128, DC, TGn])

            def rstd_gen(src_, sq_, rs_, rr_):
                k.op("act", lambda e: e.activation(out=sq_[:], in_=src_[:], func=AF.Square), reads=[src_], writes=[sq_])
                yield
                bk = pr()
                for ch in range(DC):
                    k.op("pe", lambda e: e.matmul(bk[:, 0:TGn], lhsT=ones_bf[:], rhs=sq_[:, ch, :],
                                                  start=(ch == 0), stop=(ch == DC - 1)),
                         reads=[sq_, ones_bf], writes=[bk], sig=(ch == DC - 1))
                yield
                k.op("act", lambda e: e.activation(out=rs_[:], in_=bk[:, 0:TGn], func=AF.Sqrt, bias=epscol[:],
                                                   scale=1.0 / D), reads=[bk, epscol], writes=[rs_])
                yield
                k.op("dve", lambda e: e.reciprocal(out=rr_[:], in_=rs_[:]), reads=[rs_], writes=[rr_])
                yield

            def load(i):
                s = i % NS
                tok = slice(i * TGn, (i + 1) * TGn)
                k.dma("sp", [(xg[s][:], xview(xsrc, i, tok))], xg[s], writes=[xg[s]])
                if has_m:
                    k.dma("sp", [(mg[s][:], M2[i])], mg[s], writes=[mg[s]])
                    if OPT["stats"] and OPT["use_rr"]:
                        k.dma("sp", [(rr[s][:], RR[:, tok])], rr[s], writes=[rr[s]])

            def group_gen(i, lane):
                s = i % NS
                tok = slice(i * TGn, (i + 1) * TGn)
                x_, m_, sq_, rs_, rr_ = xg[s], mg[s], sq[s], rs[s], rr[s]
                rrb = lambda: rr_[:].unsqueeze(1).to_broadcast([128, DC, TGn])
                if has_m:
                    if not (OPT["stats"] and OPT["use_rr"]):
                        yield from rstd_gen(m_, sq_, rs_, rr_)
                    if not OPT["gain_evac"]:
                        k.op("pool", lambda e: e.tensor_tensor(out=m_[:], in0=m_[:], in1=gb(kpost, lpost), op=ALU.mult),
                             reads=[m_, gains], writes=[m_])
                        yield
                    k.op("dve", lambda e: e.tensor_tensor(out=m_[:], in0=m_[:], in1=rrb(), op=ALU.mult),
                         reads=[m_, rr_], writes=[m_])
                    yield
                    k.op("dve" if lane == 0 else "pool",
                         lambda e: e.tensor_tensor(out=x_[:], in0=x_[:], in1=m_[:], op=ALU.add),
                         reads=[x_, m_], writes=[x_])
                    yield
                    k.dma("act", [(xview(xdst, i, tok), x_[:])], x_, reads=[x_])
                    yield
                if kpre is not None:
                    yield from rstd_gen(x_, sq_, rs_, rr_)
                    k.op("dve", lambda e: e.tensor_tensor(out=m_[:], in0=x_[:], in1=rrb(), op=ALU.mult),
                         reads=[x_, rr_], writes=[m_])
                    yield
                    if lane == 1:
                        for ch in range(DC):
                            k.op("act", lambda e: e.activation(out=sq_[:, ch, :], in_=m_[:, ch, :], func=AF.Copy,
                                                               scale=gcol(kpre, lpre, ch)), reads=[m_, gains], writes=[sq_])
                    else:
                        k.op("pool", lambda e: e.tensor_tensor(out=sq_[:], in0=m_[:], in1=gb(kpre, lpre), op=ALU.mult),
                             reads=[m_, gains], writes=[sq_])
                    yield
                    k.dma("act", [(fm(HT)[:, :, tok], sq_[:])], sq_, reads=[sq_])
                    yield

            NL = 2
            for i in range(min(NL, ng)):
                load(i)
            for p0 in range(0, ng, NL):
                for i in range(p0 + NL, min(p0 + 2 * NL, ng)):
                    load(i)
                gens = [group_gen(i, i - p0) for i in range(p0, min(p0 + NL, ng))]
                while gens:
                    for g_ in list(gens):
                        try:
                            next(g_)
                        except StopIteration:
                            gens.remove(g_)
            P.close()

        def gemm_out(W, A, Kd, kpost, lpost):
            KC = Kd // 128
            TW = min(T, 2048 if KC <= 16 else 1024)
            NGW = TW // 512
            P = Phase(k)
            ares = QWin(P, "ares", KC, TW)
            wblk = P.sbn("wblk", 2, [128, KC, 256], BF16)
            stage = P.sbn("stage", 4, [128, 512], F32)
            sqs = P.sbn("sqs", 4, [128, 512], BF16)
            rsw = P.sbn("rsw", 2, [128, 512], F32)
            accb = ps[0:NGW]
            pr, sr, wr, qr, rw = Rot(ps[NGW:8]), Rot(stage), Rot(wblk), Rot(sqs), Rot(rsw)
            ev = 0
            nblk = D // 256
            for w in range(T // TW):
                pre = {}
                for gi_, pb_ in enumerate(ares.parts):
                    t0_ = w * TW + gi_ * 512
                    k.dma("pool", [(pb_[:], fm(A)[:, :, t0_:t0_ + 512])], pb_, writes=[pb_])
                    if gi_ == 0:
                        pre[0] = wr()
                        k.dma("pool", [(pre[0][:], fm(W)[:, :, 0:256])], pre[0], writes=[pre[0]])
                pend = []
                for cb in range(nblk):
                    if cb in pre:
                        wb = pre.pop(cb)
                    else:
                        wb = wr()
                        k.dma("pool", [(wb[:], fm(W)[:, :, cb * 256:(cb + 1) * 256])], wb, writes=[wb])
                    for tg in range(NGW):
                        for mc in range(2):
                            chn = cb * 2 + mc
                            bk = pr()
                            for kc in range(KC):
                                k.op("pe", lambda e: e.matmul(bk[:], lhsT=wb[:, kc, mc * 128:(mc + 1) * 128],
                                                              rhs=ares.sel(tg * 512)[0][:, kc, (tg * 512) - ares.sel(tg * 512)[1]:((tg + 1) * 512) - ares.sel(tg * 512)[1]],
                                                              start=(kc == 0), stop=(kc == KC - 1)),
                                     reads=[wb, ares.sel(tg * 512)[0]], writes=[bk], sig=(kc == KC - 1))
                            while len(pend) > 1:
                                pend.pop(0)()
                            st = sr()
                            gc_ = gcol(kpost, lpost, chn)
                            if not OPT["gain_evac"]:
                                evac(ev, st[:], bk[:], [bk], [st])
                            elif ev % 2 == 0 or OPT["stats"]:
                                k.op("act", lambda e: e.activation(out=st[:], in_=bk[:], func=AF.Copy, scale=gc_),
                                     reads=[bk, gains], writes=[st])
                            else:
                                k.op("dve", lambda e: e.tensor_single_scalar(out=st[:], in_=bk[:], scalar=gc_, op=ALU.mult),
                                     reads=[bk, gains], writes=[st])
                            ev += 1
                            first = (chn == 0)
                            lastc = (chn == 2 * nblk - 1)
                            if OPT["stats"]:
                                sq_ = qr()
                                k.op("act", lambda e: e.activation(out=sq_[:], in_=bk[:], func=AF.Square),
                                     reads=[bk], writes=[sq_])

                                def red(sq_=sq_, tg=tg, first=first, lastc=lastc):
                                    k.op("pe", lambda e: e.matmul(accb[tg][:], lhsT=ones_bf[:], rhs=sq_[:], start=first,
                                                                  stop=lastc), reads=[ones_bf, sq_], writes=[accb[tg]],
                                         sig=True)
                                pend.append(red)
                            r0 = chn * 128
                            c0 = w * TW + tg * 512
                            k.dma("sp", [(M2[c0 // 256:c0 // 256 + 2, :, chn, :].rearrange("g p t -> p g t"),
                                          st[:].rearrange("p (g t) -> p g t", g=2))], st, reads=[st])
                while pend:
                    pend.pop(0)()
                for tg in range(NGW if OPT["stats"] else 0):
                    r_ = rw()
                    k.op("act", lambda e: e.activation(out=r_[:], in_=accb[tg][:], func=AF.Sqrt, bias=epscol[:],
                                                       scale=1.0 / D), reads=[accb[tg], epscol], writes=[r_])
                    k.op("dve", lambda e: e.reciprocal(out=r_[:], in_=r_[:]), reads=[r_], writes=[r_])
                    c0 = w * TW + tg * 512
                    k.dma("sp", [(RR[:, c0:c0 + 512], r_[:])], r_, reads=[r_])
            P.close()

        def ffn_in(l):
            TW, NW = c.TWF, c.NWF
            NGW = TW // 512
            P = Phase(k)
            hres = QWin(P, "hres", DC, TW, halo=True)
            wg = P.sbn("wg", 2, [128, DC, 256], BF16)
            wu = P.sbn("wu", 2, [128, DC, 256], BF16)
            gsb = P.sbn("gsb", 2, [128, TW + 2], F32)
            usb = P.sbn("usb", 2, [128, TW], F32)
            c1 = P.sbn("c1", 2, [128, TW], F32)
            ab = P.sbn("ab", 2, [128, TW], BF16)
            pr = Rot(ps)
            wi = 0
            si = 0
            wstate = [0]

            def load_w(cb):
                ncol = min(256, DFF - cb * 256)
                wg_, wu_ = wg[wstate[0] % 2], wu[wstate[0] % 2]
                wstate[0] += 1
                k.dma("pool", [(wg_[:, :, 0:ncol], fm(w_gate[l])[:, :, cb * 256:cb * 256 + ncol])], wg_, writes=[wg_])
                k.dma("pool", [(wu_[:, :, 0:ncol], fm(w_up[l])[:, :, cb * 256:cb * 256 + ncol])], wu_, writes=[wu_])
                return wg_, wu_, ncol

            for w in range(NW):
                pre = {}
                load_window(P, hres, HT, w, TW, True, after_first=lambda: pre.__setitem__(0, load_w(0)))
                for cb in range((DFF + 255) // 256):
                    wg_, wu_, ncol = pre.pop(cb) if cb in pre else load_w(cb)
                    for mc in range(ncol // 128):
                        fc = cb * 2 + mc
                        g_, u_, c_, a_ = gsb[si % 2], usb[si % 2], c1[si % 2], ab[si % 2]
                        si += 1
                        for tg in range(NGW):
                            bg_ = pr()
                            for kc in range(DC):
                                k.op("pe", lambda e: e.matmul(bg_[:], lhsT=wg_[:, kc, mc * 128:(mc + 1) * 128],
                                                              rhs=hres.sel(tg * 512)[0][:, kc, (tg * 512) - hres.sel(tg * 512)[1]:((tg + 1) * 512) - hres.sel(tg * 512)[1]],
                                                              start=(kc == 0), stop=(kc == DC - 1)),
                                     reads=[wg_, hres.sel(tg * 512)[0]], writes=[bg_], sig=(kc == DC - 1))
                            k.op("act", lambda e: e.activation(out=g_[:, 1 + tg * 512:1 + (tg + 1) * 512], in_=bg_[:],
                                                               func=AF.Copy), reads=[bg_], writes=[g_])
                            bu_ = pr()
                            for kc in range(DC):
                                k.op("pe", lambda e: e.matmul(bu_[:], lhsT=wu_[:, kc, mc * 128:(mc + 1) * 128],
                                                              rhs=hres.sel(tg * 512)[0][:, kc, (tg * 512) - hres.sel(tg * 512)[1]:((tg + 1) * 512) - hres.sel(tg * 512)[1]],
                                                              start=(kc == 0), stop=(kc == DC - 1)),
                                     reads=[wu_, hres.sel(tg * 512)[0]], writes=[bu_], sig=(kc == DC - 1))
                            k.op("dve", lambda e: e.tensor_copy(out=u_[:, tg * 512:(tg + 1) * 512], in_=bu_[:]),
                                 reads=[bu_], writes=[u_])
                        bh_ = pr()
                        for kc in range(DC):
                            k.op("pe", lambda e: e.matmul(bh_[:, 0:2], lhsT=wg_[:, kc, mc * 128:(mc + 1) * 128],
                                                          rhs=hres.sel(TW)[0][:, kc, (TW) - hres.sel(TW)[1]:(TW + 2) - hres.sel(TW)[1]],
                                                          start=(kc == 0), stop=(kc == DC - 1)),
                                 reads=[wg_, hres.sel(TW)[0]], writes=[bh_], sig=(kc == DC - 1))
                        k.op("act", lambda e: e.activation(out=g_[:, 0:1], in_=bh_[:, 0:1], func=AF.Copy),
                             reads=[bh_], writes=[g_])
                        k.op("act", lambda e: e.activation(out=g_[:, TW + 1:TW + 2], in_=bh_[:, 1:2], func=AF.Copy),
                             reads=[bh_], writes=[g_])
                        cw = (l * FC + fc) * 3
                        bcol = fconvb[:, l * FC + fc:l * FC + fc + 1]
                        k.op("dve", lambda e: e.tensor_scalar(out=c_[:], in0=g_[:, 0:TW], scalar1=fconv[:, cw:cw + 1],
                                                              scalar2=bcol, op0=ALU.mult, op1=ALU.add),
                             reads=[g_, fconv, fconvb], writes=[c_])
                        k.op("dve", lambda e: e.scalar_tensor_tensor(out=c_[:], in0=g_[:, 1:TW + 1],
                                                                     scalar=fconv[:, cw + 1:cw + 2], in1=c_[:],
                                                                     op0=ALU.mult, op1=ALU.add),
                             reads=[g_, fconv, c_], writes=[c_])
                        k.op("dve", lambda e: e.scalar_tensor_tensor(out=c_[:], in0=g_[:, 2:TW + 2],
                                                                     scalar=fconv[:, cw + 2:cw + 3], in1=c_[:],
                                                                     op0=ALU.mult, op1=ALU.add),
                             reads=[g_, fconv, c_], writes=[c_])
                        k.op("act", lambda e: e.activation(out=c_[:], in_=c_[:], func=AF.Silu), reads=[c_], writes=[c_])
                        k.op("dve", lambda e: e.tensor_tensor(out=a_[:], in0=c_[:], in1=u_[:], op=ALU.mult),
                             reads=[c_, u_], writes=[a_])
                        k.dma("sp", [(AT[fc * 128:(fc + 1) * 128, w * TW:(w + 1) * TW], a_[:])], a_, reads=[a_])
            P.close()

        def c_in(j):
            TW, NW = c.TWF, c.NWF
            NGW = TW // 512
            P = Phase(k)
            hres = QWin(P, "hres", DC, TW, halo=True)
            w3 = P.sbn("w3", 2, [128, DC, 3, 256], BF16)
            psb = P.sbn("psb", 2, [128, TW + 2], F32)
            bgs = P.sbn("bgs", 2, [128, TW], F32)
            c1 = P.sbn("c1", 2, [128, TW], F32)
            ob = P.sbn("ob", 2, [128, TW], BF16)
            cgt = P.sbn("cgt", 2, [128, 512], F32)
            hct = P.sbn("hct", 2, [128, 2], F32)
            pr = Rot(ps)
            wi = si = ci = 0
            Wc = w_cin[j]
            wstate = [0]

            def load_w(cb):
                w_ = w3[wstate[0] % 2]
                wstate[0] += 1
                k.dma("pool", [(w_[:, :, q, :], fm(Wc)[:, :, q * D + cb * 256:q * D + (cb + 1) * 256])
                               for q in range(3)], w_, writes=[w_])
                return w_

            for w in range(NW):
                pre = {}
                load_window(P, hres, HT, w, TW, True, after_first=lambda: pre.__setitem__(0, load_w(0)))
                for cb in range(D // 256):
                    w_ = pre.pop(cb) if cb in pre else load_w(cb)
                    for mc in range(2):
                        fcj = cb * 2 + mc
                        p_, b_, c_, o_ = psb[si % 2], bgs[si % 2], c1[si % 2], ob[si % 2]
                        hc_ = hct[si % 2]
                        si += 1
                        cs_ = slice(mc * 128, (mc + 1) * 128)
                        for tg in range(NGW):
                            ts_ = slice(tg * 512, (tg + 1) * 512)
                            bc_, bx_, bb_ = pr(), pr(), pr()
                            for q, bk in ((1, bc_), (2, bx_), (0, bb_)):
                                for kc in range(DC):
                                    k.op("pe", lambda e: e.matmul(bk[:], lhsT=w_[:, kc, q, cs_], rhs=hres.sel(ts_.start)[0][:, kc, (ts_.start) - hres.sel(ts_.start)[1]:(ts_.stop) - hres.sel(ts_.start)[1]],
                                                                  start=(kc == 0), stop=(kc == DC - 1)),
                                         reads=[w_, hres.sel(ts_.start)[0]], writes=[bk], sig=(kc == DC - 1))
                            ct = cgt[ci % 2]
                            ci += 1
                            k.op("act", lambda e: e.activation(out=ct[:], in_=bc_[:], func=AF.Copy), reads=[bc_], writes=[ct])
                            k.op("dve", lambda e: e.tensor_tensor(out=p_[:, 1 + tg * 512:1 + (tg + 1) * 512], in0=ct[:],
                                                                  in1=bx_[:], op=ALU.mult), reads=[ct, bx_], writes=[p_])
                            k.op("act", lambda e: e.activation(out=b_[:, ts_], in_=bb_[:], func=AF.Copy),
                                 reads=[bb_], writes=[b_])
                        bh_ = pr()
                        for q, c0 in ((1, 0), (2, 8)):
                            for kc in range(DC):
                                k.op("pe", lambda e: e.matmul(bh_[:, c0:c0 + 2], lhsT=w_[:, kc, q, cs_],
                                                              rhs=hres.sel(TW)[0][:, kc, (TW) - hres.sel(TW)[1]:(TW + 2) - hres.sel(TW)[1]],
                                                              start=(kc == 0), stop=(kc == DC - 1)),
                                     reads=[w_, hres.sel(TW)[0]], writes=[bh_], sig=(kc == DC - 1))
                        k.op("act", lambda e: e.activation(out=hc_[:], in_=bh_[:, 0:2], func=AF.Copy),
                             reads=[bh_], writes=[hc_])
                        k.op("dve", lambda e: e.tensor_tensor(out=p_[:, 0:1], in0=hc_[:, 0:1], in1=bh_[:, 8:9], op=ALU.mult),
                             reads=[hc_, bh_], writes=[p_])
                        k.op("dve", lambda e: e.tensor_tensor(out=p_[:, TW + 1:TW + 2], in0=hc_[:, 1:2], in1=bh_[:, 9:10],
                                                              op=ALU.mult), reads=[hc_, bh_], writes=[p_])
                        cw = (j * DC + fcj) * 3
                        k.op("dve", lambda e: e.tensor_single_scalar(out=c_[:], in_=p_[:, 0:TW], scalar=cconv[:, cw:cw + 1],
                                                                     op=ALU.mult), reads=[p_, cconv], writes=[c_])
                        k.op("dve", lambda e: e.scalar_tensor_tensor(out=c_[:], in0=p_[:, 1:TW + 1],
                                                                     scalar=cconv[:, cw + 1:cw + 2], in1=c_[:],
                                                                     op0=ALU.mult, op1=ALU.add),
                             reads=[p_, cconv, c_], writes=[c_])
                        k.op("dve", lambda e: e.scalar_tensor_tensor(out=c_[:], in0=p_[:, 2:TW + 2],
                                                                     scalar=cconv[:, cw + 2:cw + 3], in1=c_[:],
                                                                     op0=ALU.mult, op1=ALU.add),
                             reads=[p_, cconv, c_], writes=[c_])
                        k.op("dve", lambda e: e.tensor_tensor(out=o_[:], in0=c_[:], in1=b_[:], op=ALU.mult),
                             reads=[c_, b_], writes=[o_])
                        k.dma("sp", [(MT[fcj * 128:(fcj + 1) * 128, w * TW:(w + 1) * TW], o_[:])], o_, reads=[o_])
            P.close()

        def ab_in(j):
            TW = min(T, 2048)
            NGW = TW // 512
            P = Phase(k)
            hres = QWin(P, "hres", DC, TW)
            wblk = P.sbn("wblk", 2, [128, DC, 256], BF16)
            stq = P.sbn("stq", 4, [128, 512], BF16)
            stv = P.sbn("stv", 4, [128, 256], BF16)
            fsb = P.sbn("fsb", 2, [128, 2, 512], BF16)
            sty = P.sbn("sty", 4, [128, 512], BF16)
            pr, qr, vr, fr, yr, wr = Rot(ps), Rot(stq), Rot(stv), Rot(fsb), Rot(sty), Rot(wblk)
            ev = 0
            Wc = w_abin[j]
            def load_w(cb):
                wb = wr()
                k.dma("pool", [(wb[:], fm(Wc)[:, :, cb * 256:cb * 256 + 256])], wb, writes=[wb])
                return wb

            for w in range(T // TW):
                pre = {}
                load_window(P, hres, HT, w, TW, False, after_first=lambda: pre.__setitem__(0, load_w(0)))
                for cb in range(4 * AW // 256):
                    col0 = cb * 256
                    wb = pre.pop(cb) if cb in pre else load_w(cb)
                    kind = col0 // AW
                    if kind in (0, 1):
                        dst = QT if kind == 0 else KT
                        for mc in range(2):
                            r0 = col0 - kind * AW + mc * 128
                            for tg in range(NGW):
                                bk = pr()
                                for kc in range(DC):
                                    k.op("pe", lambda e: e.matmul(bk[:], lhsT=wb[:, kc, mc * 128:(mc + 1) * 128],
                                                                  rhs=hres.sel(tg * 512)[0][:, kc, (tg * 512) - hres.sel(tg * 512)[1]:((tg + 1) * 512) - hres.sel(tg * 512)[1]],
                                                                  start=(kc == 0), stop=(kc == DC - 1)),
                                         reads=[wb, hres.sel(tg * 512)[0]], writes=[bk], sig=(kc == DC - 1))
                                st = qr()
                                evac(ev, st[:], bk[:], [bk], [st])
                                ev += 1
                                t0 = w * TW + tg * 512
                                k.dma("sp", [(dst[r0:r0 + 128, t0:t0 + 512], st[:])], st, reads=[st])
                    elif kind == 2:
                        v0 = col0 - 2 * AW
                        for tt in range(TW // 128):
                            bk = pr()
                            for kc in range(DC):
                                k.op("pe", lambda e: e.matmul(bk[:, 0:256], lhsT=hres.sel(tt * 128)[0][:, kc, (tt * 128) - hres.sel(tt * 128)[1]:((tt + 1) * 128) - hres.sel(tt * 128)[1]],
                                                              rhs=wb[:, kc, :], start=(kc == 0), stop=(kc == DC - 1)),
                                     reads=[wb, hres.sel(tt * 128)[0]], writes=[bk], sig=(kc == DC - 1))
                            st = vr()
                            evac(ev, st[:], bk[:, 0:256], [bk], [st])
                            ev += 1
                            t0 = w * TW + tt * 128
                            k.dma("sp", [(VV[t0:t0 + 128, v0:v0 + 256], st[:])], st, reads=[st])
                    else:
                        g = (col0 - 3 * AW) // 256
                        for tg in range(NGW):
                            f_ = fr()
                            for jj in range(2):
                                bk = pr()
                                for kc in range(DC):
                                    k.op("pe", lambda e: e.matmul(bk[:], lhsT=wb[:, kc, jj * 128:(jj + 1) * 128],
                                                                  rhs=hres.sel(tg * 512)[0][:, kc, (tg * 512) - hres.sel(tg * 512)[1]:((tg + 1) * 512) - hres.sel(tg * 512)[1]],
                                                                  start=(kc == 0), stop=(kc == DC - 1)),
                                         reads=[wb, hres.sel(tg * 512)[0]], writes=[bk], sig=(kc == DC - 1))
                                evac(ev, f_[:, jj, :], bk[:], [bk], [f_])
                                ev += 1
                            for tt in range(4):
                                bk = pr()
                                for jj in range(2):
                                    k.op("pe", lambda e: e.matmul(bk[:], lhsT=f_[:, jj, tt * 128:(tt + 1) * 128],
                                                                  rhs=csc[:, jj, :], start=(jj == 0), stop=(jj == 1)),
                                         reads=[f_, csc], writes=[bk], sig=(jj == 1))
                                st = yr()
                                evac(ev, st[:], bk[:], [bk], [st])
                                ev += 1
                                cc = (w * TW + tg * 512 + tt * 128) // 128
                                k.dma("sp", [(YY[g * 2:g * 2 + 2, :, cc, :].rearrange("a p b -> p a b"),
                                              st[:].rearrange("p (a b) -> p a b", a=2))], st, reads=[st])
            P.close()

        def attention(j):
            li = lambda_init_fn(2 * j)
            P = Phase(k)
            qh = P.sbn("qh", 2, [128, T], BF16)
            kh = P.sbn("kh", 2, [128, T], BF16)
            vh = P.sbn("vh", 2, [128, NKC, 128], BF16)
            Gt = P.sb("Gt", [128, H, GW], BF16)
            bc = P.sb("bc", [128, H * NKC * NTG], F32)
            Eb = P.sbn("Eb", 6, [128, 512], BF16)
            zs = [P.sbn("zs0_", 2, [128, 512], F32), P.sbn("zs1_", 2, [128, 512], F32)]
            osb = [P.sbn("os0_", 2, [128, 512], F32), P.sbn("os1_", 2, [128, 512], F32)]
            lnb = P.sbn("lnb", 2, [128, 512], F32)
            sqb = P.sbn("sqb", 2, [128, 512], BF16)
            obf = P.sbn("obf", 2, [128, 512], BF16)
            sr = Rot(ps[0:4])
            er = Rot(Eb)
            Ob = [ps[4], ps[5]]
            Zb = [ps[6], ps[7]]
            k.dma("sp", [(Gt[:, h, :], bass.AP(UU.tensor, h * 128 * LU + 127, [[LU - 1, 128], [1, GW]])) for h in range(H)],
                  Gt, writes=[Gt])
            NB = NKC * NTG
            for h in range(H):
                k.op("dve", lambda e: e.scalar_tensor_tensor(out=bc[:, h * NB:(h + 1) * NB], in0=lmask[:],
                                                             scalar=clr[:, h:h + 1], in1=amask[:],
                                                             op0=ALU.mult, op1=ALU.add),
                     reads=[lmask, clr, amask], writes=[bc])
                k.op("dve", lambda e: e.scalar_tensor_tensor(out=bc[:, h * NB:(h + 1) * NB], in0=rmask[:],
                                                             scalar=clr[:, 8 + h:9 + h], in1=bc[:, h * NB:(h + 1) * NB],
                                                             op0=ALU.mult, op1=ALU.add),
                     reads=[rmask, clr, bc], writes=[bc])

            def load_head(h):
                q_, k_, v_ = qh[h % 2], kh[h % 2], vh[h % 2]
                k.dma("sp", [(q_[:], QT[h * 128:(h + 1) * 128, :])], q_, writes=[q_])
                k.dma("sp", [(k_[:], KT[h * 128:(h + 1) * 128, :])], k_, writes=[k_])
                k.dma("sp", [(v_[:], VV.rearrange("(c p) e -> p c e", p=128)[:, :, h * 128:(h + 1) * 128])], v_, writes=[v_])

            fi = 0
            pending = []
            load_head(0)
            for h in range(H):
                q_, k_, v_ = qh[h % 2], kh[h % 2], vh[h % 2]
                for qg in range(NTG):
                    if qg == 1 and h + 1 < H:
                        load_head(h + 1)
                    ebufs = {}
                    for jx in range(NKC + 1):
                        if jx == min(6, NKC - 1) and pending:
                            pending.pop(0)()
                        if jx < NKC:
                            kc = jx
                            near = (4 * qg - 1 <= kc <= 4 * qg + 4)
                            sbks = [sr(), sr()]
                            for m in range(2):
                                ms = slice(m * 64, (m + 1) * 64)
                                sbk = sbks[m]
                                k.op("pe", lambda e: e.matmul(sbk[:], lhsT=k_[ms, kc * 128:(kc + 1) * 128],
                                                              rhs=q_[ms, qg * 512:(qg + 1) * 512], start=True,
                                                              stop=not near),
                                     reads=[k_, q_], writes=[sbk], sig=not near)
                            if near:
                                off = 512 - (128 * kc - 512 * qg)
                                for m in range(2):
                                    sbk = sbks[m]
                                    k.op("pe", lambda e: e.matmul(sbk[:], lhsT=ident[:], rhs=Gt[:, h, off:off + 512],
                                                                  start=False, stop=True),
                                         reads=[ident, Gt], writes=[sbk], sig=True)
                            bi = (h * NKC + kc) * NTG + qg
                            for m in range(2):
                                eb = er()
                                ebufs[(m, kc)] = eb
                                sbk = sbks[m]
                                k.op("act", lambda e: e.activation(out=eb[:], in_=sbk[:], func=AF.Exp,
                                                                   bias=bc[:, bi:bi + 1], scale=0.125),
                                     reads=[sbk, bc], writes=[eb])
                        if jx >= 1:
                            kc = jx - 1
                            for m in range(2):
                                eb = ebufs.pop((m, kc))
                                k.op("pe", lambda e: e.matmul(Ob[m][:], lhsT=v_[:, kc, :], rhs=eb[:], start=(kc == 0),
                                                              stop=(kc == NKC - 1)), reads=[v_, eb], writes=[Ob[m]],
                                     sig=(kc == NKC - 1))
                                k.op("pe", lambda e: e.matmul(Zb[m][:], lhsT=ones_bf[:], rhs=eb[:], start=(kc == 0),
                                                              stop=(kc == NKC - 1)), reads=[ones_bf, eb], writes=[Zb[m]],
                                     sig=True)
                    s_ = fi % 2
                    fi += 1
                    z0, z1, o0, o1 = zs[0][s_], zs[1][s_], osb[0][s_], osb[1][s_]
                    ln_, sq_, ob_ = lnb[s_], sqb[s_], obf[s_]
                    k.op("dve", lambda e: e.tensor_copy(out=z0[:], in_=Zb[0][:]), reads=[Zb[0]], writes=[z0])
                    k.op("dve", lambda e: e.tensor_copy(out=o0[:], in_=Ob[0][:]), reads=[Ob[0]], writes=[o0])
                    k.op("dve", lambda e: e.tensor_copy(out=z1[:], in_=Zb[1][:]), reads=[Zb[1]], writes=[z1])
                    k.op("dve", lambda e: e.tensor_copy(out=o1[:], in_=Ob[1][:]), reads=[Ob[1]], writes=[o1])
                    k.op("dve", lambda e: e.reciprocal(out=z0[:], in_=z0[:]), reads=[z0], writes=[z0])
                    k.op("dve", lambda e: e.reciprocal(out=z1[:], in_=z1[:]), reads=[z1], writes=[z1])
                    k.op("dve", lambda e: e.tensor_tensor(out=o0[:], in0=o0[:], in1=z0[:], op=ALU.mult),
                         reads=[o0, z0], writes=[o0])
                    k.op("dve", lambda e: e.tensor_tensor(out=o1[:], in0=o1[:], in1=z1[:], op=ALU.mult),
                         reads=[o1, z1], writes=[o1])
                    k.op("dve", lambda e: e.scalar_tensor_tensor(out=o0[:], in0=o1[:], scalar=neglam[:, j:j + 1],
                                                                 in1=o0[:], op0=ALU.mult, op1=ALU.add),
                         reads=[o1, neglam, o0], writes=[o0])
                    k.op("dve", lambda e: e.tensor_tensor(out=sq_[:], in0=o0[:], in1=o0[:], op=ALU.mult),
                         reads=[o0], writes=[sq_])

                    def part2(o0=o0, ln_=ln_, sq_=sq_, ob_=ob_, h=h, qg=qg):
                        sbk = sr()
                        k.op("pe", lambda e: e.matmul(sbk[:], lhsT=ones_bf[:], rhs=sq_[:], start=True, stop=True),
                             reads=[ones_bf, sq_], writes=[sbk])
                        k.op("act", lambda e: e.activation(out=ln_[:], in_=sbk[:], func=AF.Ln, bias=epscol[:],
                                                           scale=1.0 / 128), reads=[sbk, epscol], writes=[ln_])
                        k.op("act", lambda e: e.activation(out=ln_[:], in_=ln_[:], func=AF.Exp, scale=-0.5),
                             reads=[ln_], writes=[ln_])
                        k.op("dve", lambda e: e.tensor_tensor(out=o0[:], in0=o0[:], in1=ln_[:], op=ALU.mult),
                             reads=[o0, ln_], writes=[o0])
                        k.op("dve", lambda e: e.tensor_scalar(out=ob_[:], in0=o0[:], scalar1=subln[:, j:j + 1],
                                                              scalar2=float(1.0 - li), op0=ALU.mult, op1=ALU.mult),
                             reads=[o0, subln], writes=[ob_])
                        k.dma("sp", [(MT[h * 128:(h + 1) * 128, qg * 512:(qg + 1) * 512], ob_[:])], ob_, reads=[ob_])

                    pending.append(part2)
            while pending:
                pending.pop(0)()
            P.close()

        def seqdft(j):
            P = Phase(k)
            cs = P.sbn("cs", 2, [128, NKC, 512], BF16)
            ss = P.sbn("ss", 2, [128, NKC, 512], BF16)
            yp = P.sbn("yp", 3, [128, NKC, 2, 128], BF16)
            st4 = P.sbn("st", 4, [128, 512], BF16)
            pr, sr_, yr = Rot(ps), Rot(st4), Rot(yp)
            ev = 0
            for sbi in range(NTG):
                cs_, ss_ = cs[sbi % 2], ss[sbi % 2]
                k.dma("sp", [(cs_[:], fm(dftc_d)[:, :, sbi * 512:(sbi + 1) * 512])], cs_, writes=[cs_])
                k.dma("sp", [(ss_[:], fm(dfts_d)[:, :, sbi * 512:(sbi + 1) * 512])], ss_, writes=[ss_])
                for g in range(NG):
                    for jj in range(2):
                        y_ = yr()
                        k.dma("sp", [(y_[:], YY[g * 2 + jj].rearrange("p c (a b) -> p c a b", a=2))], y_, writes=[y_])
                        bk = pr()
                        for cc in range(NKC):
                            k.op("pe", lambda e: e.matmul(bk[:], lhsT=y_[:, cc, 0, :], rhs=cs_[:, cc, :],
                                                          start=(cc == 0), stop=False),
                                 reads=[y_, cs_], writes=[bk], sig=False)
                            k.op("pe", lambda e: e.matmul(bk[:], lhsT=y_[:, cc, 1, :], rhs=ss_[:, cc, :],
                                                          start=False, stop=(cc == NKC - 1)),
                                 reads=[y_, ss_], writes=[bk], sig=(cc == NKC - 1))
                        st = sr_()
                        evac(ev, st[:], bk[:], [bk], [st])
                        ev += 1
                        r0 = AW + (g * 2 + jj) * 128
                        k.dma("act", [(MT[r0:r0 + 128, sbi * 512:(sbi + 1) * 512], st[:])], st, reads=[st])
            P.close()

        norm_pass(xT, False, None, None, None, 0, 0)
        for i in range(L):
            j = i // 2
            if i % 2 == 0:
                ab_in(j)
                attention(j)
                seqdft(j)
                gemm_out(w_about[j], MT, D, 1, i)
            else:
                c_in(j)
                gemm_out(w_cout[j], MT, D, 1, i)
            norm_pass(xT if i == 0 else XS, True, XS, 1, i, 2, i)
            ffn_in(i)
            gemm_out(w_down[i], AT, DFF, 3, i)
            last = (i == L - 1)
            if last:
                norm_pass(XS, True, yT, 3, i, None, None)
            else:
                norm_pass(XS, True, XS, 3, i, 0, i + 1)
        nins = k.nins
    return nc, nins


def rel_bucket_np(rel):
    nb = 16
    ret = np.where(rel > 0, nb, 0)
    n = np.abs(rel)
    max_exact = nb // 2
    nf = np.maximum(n, 1).astype(np.float32)
    large = max_exact + (np.log(nf / np.float32(max_exact)) / np.float32(math.log(128 / max_exact))
                         * np.float32(nb - max_exact)).astype(np.int32)
    large = np.minimum(large, nb - 1)
    return ret + np.where(n < max_exact, n, large)


def core_consts(cfg, seqlens):
    T, NKC, NTG = cfg.T, cfg.NKC, cfg.NTG
    starts = np.cumsum([0] + list(seqlens))
    seq_of = np.zeros(T, np.int64)
    for si, (a, b) in enumerate(zip(starts[:-1], starts[1:])):
        seq_of[a:b] = si
    dc = np.zeros((T, T), np.float64)
    ds = np.zeros((T, T), np.float64)
    for a, S in zip(starts[:-1], seqlens):
        idx = np.arange(S, dtype=np.int64)
        m = (idx[:, None] * idx[None, :]) % S
        ang = 2.0 * np.pi * m / S
        dc[a:a + S, a:a + S] = np.cos(ang) / math.sqrt(S)
        ds[a:a + S, a:a + S] = np.sin(ang) / math.sqrt(S)
    am = np.zeros((NKC, NTG), np.float32)
    for kc in range(NKC):
        for qg in range(NTG):
            if seq_of[kc * 128] != seq_of[qg * 512]:
                am[kc, qg] = NEG
    keep = np.zeros(cfg.NWF + 1, np.float32)
    for b in range(1, cfg.NWF):
        t = b * cfg.TWF
        keep[b] = 1.0 if seq_of[t] == seq_of[t - 1] else 0.0
    rep = lambda v: np.ascontiguousarray(np.broadcast_to(v.reshape(1, -1), (128, v.size))).astype(np.float32)
    return {"dftc": dc.astype(NPBF), "dfts": ds.astype(NPBF), "amask": rep(am), "keep": rep(keep)}


def shared_consts(cfg):
    NKC, NTG = cfg.NKC, cfg.NTG
    lm = np.zeros((NKC, NTG), np.float32)
    rm = np.zeros((NKC, NTG), np.float32)
    for kc in range(NKC):
        for qg in range(NTG):
            if kc < 4 * qg - 1:
                lm[kc, qg] = 1.0
            if kc > 4 * qg + 4:
                rm[kc, qg] = 1.0
    rep = lambda v: np.ascontiguousarray(np.broadcast_to(v.reshape(1, -1), (128, v.size))).astype(np.float32)
    rel = 639 - np.arange(LU)
    bk = rel_bucket_np(rel)
    oh = (bk[None, :] == np.arange(32)[:, None]).astype(np.float32)
    cidx = np.arange(256, dtype=np.int64)
    ang = 2.0 * np.pi * ((cidx[:, None] * cidx[None, :]) % 256) / 256.0
    cc_, ss_ = np.cos(ang), -np.sin(ang)
    csc = np.concatenate([cc_[:, 0:128], ss_[:, 0:128], cc_[:, 128:256], ss_[:, 128:256]], axis=1) / 16.0
    return {"lmask": rep(lm), "rmask": rep(rm), "oh": oh, "csc": csc.astype(NPBF),
            "ident": np.eye(128, dtype=np.float32).astype(NPBF)}


def param_layout(cfg, p):
    L, DC, FC, NAB, NCL, D, DFF = cfg.L, cfg.DC, cfg.FC, cfg.NAB, cfg.NCL, cfg.D, cfg.DFF
    f = lambda a: np.ascontiguousarray(np.asarray(a, dtype=np.float32))
    gains = np.stack([f(p["norm_pre_mix"]), f(p["norm_post_mix"]), f(p["norm_pre_ffn"]), f(p["norm_post_ffn"])])
    gains = gains.reshape(4, L, DC, 128).transpose(3, 0, 1, 2).reshape(128, 4 * L * DC)
    cconv = f(p["c_conv"]).reshape(NCL, 3, DC, 128).transpose(3, 0, 2, 1).reshape(128, NCL * DC * 3)
    fconv = f(p["ffn_conv"]).reshape(L, 3, FC, 128).transpose(3, 0, 2, 1).reshape(128, L * FC * 3)
    fconvb = f(p["ffn_conv_b"]).reshape(L, FC, 128).transpose(2, 0, 1).reshape(128, L * FC)
    subln = f(p["ab_subln"]).T
    lamin = np.stack([f(p["ab_lambda_q1"]), f(p["ab_lambda_k1"]), f(p["ab_lambda_q2"]), f(p["ab_lambda_k2"])])
    relb = np.zeros((32, 8), np.float32)
    rb = f(p["rel_bias"])
    relb[:, :rb.shape[1]] = rb
    return {
        "w_abin": f(p["ab_w_in"]), "w_about": f(p["ab_w_out"]), "w_cin": f(p["c_w_in"]), "w_cout": f(p["c_w_out"]),
        "w_gate": f(p["ffn_w_gate"]), "w_up": f(p["ffn_w_up"]), "w_down": f(p["ffn_w_down"]),
        "gains": f(gains), "cconv": f(cconv), "fconv": f(fconv), "fconvb": f(fconvb), "subln": f(subln),
        "lamin": f(lamin.reshape(1, -1)), "relb": relb,
    }


def run(cfg, params, core_x, core_seqlens):
    nc, nins = build(cfg)
    base = dict(param_layout(cfg, params))
    base.update(shared_consts(cfg))
    cc_cache = {}
    in_maps = []
    for x, sl in zip(core_x, core_seqlens):
        key = tuple(sl)
        if key not in cc_cache:
            cc_cache[key] = core_consts(cfg, sl)
        m = dict(base)
        m.update(cc_cache[key])
        m["xT"] = np.ascontiguousarray(np.asarray(x, dtype=np.float32).T)
        in_maps.append(m)
    res = run_bass_kernel_spmd(nc, in_maps, core_ids=list(range(len(in_maps))))
    return [np.ascontiguousarray(r["yT"].T) for r in res.results]


def kernel(**inputs):
    xp = np.asarray(inputs["x_prompt"], dtype=np.float32)
    xs = np.asarray(inputs["x_sample"], dtype=np.float32)
    B, S, D = xp.shape
    DB, DS, _ = xs.shape
    L = int(np.asarray(inputs["norm_pre_mix"]).shape[0])
    cfg = Cfg(D=D, T=S, L=L, TWF=2048)
    assert 2 * DS == S and B == 2 and DB == 8
    core_x, core_sl = [], []
    for b in range(B):
        core_x.append(xp[b])
        core_sl.append([S])
    for i in range(DB // 2):
        core_x.append(np.concatenate([xs[2 * i], xs[2 * i + 1]], axis=0))
        core_sl.append([DS, DS])
    while len(core_x) < 8:
        core_x.append(np.zeros_like(core_x[-1]))
        core_sl.append(core_sl[-1])
    outs = run(cfg, inputs, core_x, core_sl)
    y_prompt = np.stack([outs[b] for b in range(B)]).astype(np.float32)
    ys = []
    for i in range(DB // 2):
        o = outs[B + i]
        ys.append(o[:DS])
        ys.append(o[DS:])
    y_sample = np.stack(ys).astype(np.float32)
    return (y_prompt, y_sample)
```
